# Optimizing a Trainium2 kernel written in Bass

```python
import math
import jax, jax.numpy as jnp
from jax import lax
import numpy as np

D_MODEL = 1024
BATCH = 32
SEQ = 2048
DEPTH = 1

CHUNK = 64
Q_BLOCK = 128
HEAD_DIM = 64
MIX_WIDTH = D_MODEL
WIDTH_A = MIX_WIDTH // 2
WIDTH_B = MIX_WIDTH - WIDTH_A
N_HEADS_A = WIDTH_A // HEAD_DIM
N_HEADS_B = WIDTH_B // HEAD_DIM
DIFF_DIM = HEAD_DIM // 2
LEFT_CHUNKS = 8
BAND = (LEFT_CHUNKS + 1) * CHUNK
MAX_REL_DIST = 256
D_FF = 4 * D_MODEL
ROPE_THETA = 10000.0
EPS = 1e-6
QA_COLS = N_HEADS_A * 2 * DIFF_DIM
KA_COLS = N_HEADS_A * 2 * DIFF_DIM
VA_COLS = N_HEADS_A * HEAD_DIM
QB_COLS = N_HEADS_B * HEAD_DIM
KB_COLS = N_HEADS_B * HEAD_DIM
VB_COLS = N_HEADS_B * HEAD_DIM
IN_COLS = QA_COLS + KA_COLS + VA_COLS + QB_COLS + KB_COLS + VB_COLS

kernel_name = "hybrid_diffattn_chunkrel_block"


def rmsnorm(x, g):
    xf = x.astype(jnp.float32)
    y = xf * lax.rsqrt(jnp.mean(xf * xf, axis=-1, keepdims=True) + EPS)
    return y.astype(x.dtype) * g


def rope(x, pos):
    d = x.shape[-1]
    half = d // 2
    inv_freq = ROPE_THETA ** (-jnp.arange(half, dtype=jnp.float32) / half)
    ang = pos.astype(jnp.float32)[:, None] * inv_freq[None, :]
    cos = jnp.cos(ang)[:, None, :]
    sin = jnp.sin(ang)[:, None, :]
    xf = x.astype(jnp.float32)
    x1, x2 = xf[..., :half], xf[..., half:]
    out = jnp.concatenate([x1 * cos - x2 * sin, x2 * cos + x1 * sin], axis=-1)
    return out.astype(x.dtype)


def diff_attention(q, k, v, lam, subln_g, lam_init):
    S = q.shape[1]
    scale = DIFF_DIM ** -0.5
    chunk_id = jnp.arange(S) // CHUNK
    outs = []
    for i in range(S // Q_BLOCK):
        q0, q1 = i * Q_BLOCK, (i + 1) * Q_BLOCK
        qb, kb, vb = q[:, q0:q1], k[:, :q1], v[:, :q1]
        s = jnp.einsum('bqhcd,bkhcd->bhcqk', qb, kb).astype(jnp.float32) * scale
        mask = chunk_id[None, :q1] <= chunk_id[q0:q1, None]
        s = jnp.where(mask, s, -jnp.inf)
        p = jax.nn.softmax(s, axis=-1)
        a = p[:, :, 0] - lam * p[:, :, 1]
        outs.append(jnp.einsum('bhqk,bkhd->bqhd', a.astype(vb.dtype), vb))
    o = jnp.concatenate(outs, axis=1)
    return rmsnorm(o, subln_g) * (1.0 - lam_init)


def chunked_rel_attention(q, k, v, rel_table):
    B, S, H, D = q.shape
    nc = S // CHUNK
    pad = LEFT_CHUNKS * CHUNK
    scale = D ** -0.5
    kp = jnp.pad(k, ((0, 0), (pad, 0), (0, 0), (0, 0)))
    vp = jnp.pad(v, ((0, 0), (pad, 0), (0, 0), (0, 0)))
    q_local = pad + jnp.arange(CHUNK)
    k_local = jnp.arange(BAND)
    dist = jnp.clip(q_local[:, None] - k_local[None, :], -MAX_REL_DIST, MAX_REL_DIST) + MAX_REL_DIST
    bias = rel_table[:, dist].astype(jnp.float32)
    qc = q.reshape(B, nc, CHUNK, H, D).transpose(1, 0, 2, 3, 4)

    def one_chunk(args):
        c, qi = args
        start = c * CHUNK
        kb = lax.dynamic_slice_in_dim(kp, start, BAND, axis=1)
        vb = lax.dynamic_slice_in_dim(vp, start, BAND, axis=1)
        s = jnp.einsum('bqhd,bkhd->bhqk', qi, kb).astype(jnp.float32) * scale + bias
        valid = (start + k_local) >= pad
        s = jnp.where(valid[None, None, None, :], s, -jnp.inf)
        p = jax.nn.softmax(s, axis=-1)
        return jnp.einsum('bhqk,bkhd->bqhd', p.astype(vb.dtype), vb)

    o = lax.map(one_chunk, (jnp.arange(nc), qc))
    return o.transpose(1, 0, 2, 3, 4).reshape(B, S, H, D)


def setup_inputs(seed: int = 0) -> dict:
    key = jax.random.key(seed)
    ks = jax.random.split(key, 16)
    f32 = jnp.float32
    n = lambda k, shape, s: jax.random.normal(k, shape, f32) * s
    return {
        "x": n(ks[0], (BATCH, SEQ, D_MODEL), 1.0),
        "w_in": n(ks[1], (DEPTH, D_MODEL, IN_COLS), D_MODEL ** -0.5),
        "w_out": n(ks[2], (DEPTH, MIX_WIDTH, D_MODEL), MIX_WIDTH ** -0.5),
        "norm1_g": 1.0 + n(ks[3], (DEPTH, D_MODEL), 0.02),
        "norm2_g": 1.0 + n(ks[4], (DEPTH, D_MODEL), 0.02),
        "final_g": 1.0 + n(ks[5], (D_MODEL,), 0.02),
        "subln_g": 1.0 + n(ks[6], (DEPTH, HEAD_DIM), 0.02),
        "lambda_q1": n(ks[7], (DEPTH, DIFF_DIM), 0.1),
        "lambda_k1": n(ks[8], (DEPTH, DIFF_DIM), 0.1),
        "lambda_q2": n(ks[9], (DEPTH, DIFF_DIM), 0.1),
        "lambda_k2": n(ks[10], (DEPTH, DIFF_DIM), 0.1),
        "rel_bias": n(ks[11], (DEPTH, N_HEADS_B, 2 * MAX_REL_DIST + 1), 0.1),
        "w_ff1": n(ks[12], (DEPTH, D_MODEL, D_FF), D_MODEL ** -0.5),
        "w_ff2": n(ks[13], (DEPTH, D_FF, D_MODEL), D_FF ** -0.5),
    }


def reference(x, w_in, w_out, norm1_g, norm2_g, final_g, subln_g, lambda_q1, lambda_k1,
              lambda_q2, lambda_k2, rel_bias, w_ff1, w_ff2):
    B, S, _ = x.shape
    pos = jnp.arange(S)
    splits = np.cumsum([QA_COLS, KA_COLS, VA_COLS, QB_COLS, KB_COLS]).tolist()
    for l in range(DEPTH):
        lam_init = 0.8 - 0.6 * math.exp(-0.3 * l)
        h = rmsnorm(x, norm1_g[l])
        proj = h @ w_in[l]
        qa, ka, va, qb, kb, vb = jnp.split(proj, splits, axis=-1)
        qa = rope(qa.reshape(B, S, N_HEADS_A * 2, DIFF_DIM), pos).reshape(B, S, N_HEADS_A, 2, DIFF_DIM)
        ka = rope(ka.reshape(B, S, N_HEADS_A * 2, DIFF_DIM), pos).reshape(B, S, N_HEADS_A, 2, DIFF_DIM)
        va = va.reshape(B, S, N_HEADS_A, HEAD_DIM)
        lam = (jnp.exp(jnp.sum(lambda_q1[l].astype(jnp.float32) * lambda_k1[l].astype(jnp.float32)))
               - jnp.exp(jnp.sum(lambda_q2[l].astype(jnp.float32) * lambda_k2[l].astype(jnp.float32)))
               + lam_init)
        oa = diff_attention(qa, ka, va, lam, subln_g[l], lam_init)
        qb = qb.reshape(B, S, N_HEADS_B, HEAD_DIM)
        kb = kb.reshape(B, S, N_HEADS_B, HEAD_DIM)
        vb = vb.reshape(B, S, N_HEADS_B, HEAD_DIM)
        ob = chunked_rel_attention(qb, kb, vb, rel_bias[l])
        mix = jnp.concatenate([oa.reshape(B, S, WIDTH_A), ob.reshape(B, S, WIDTH_B)], axis=-1)
        x = x + mix @ w_out[l]
        h = rmsnorm(x, norm2_g[l])
        x = x + jnp.square(jax.nn.relu(h @ w_ff1[l])) @ w_ff2[l]
    return rmsnorm(x, final_g)
```

```python
import contextlib
import numpy as np
import ml_dtypes
import concourse.bass as bass
import concourse.mybir as mybir
from concourse.bass_utils import run_bass_kernel_spmd

F32 = mybir.dt.float32
BF16 = mybir.dt.bfloat16
AF = mybir.ActivationFunctionType
ALU = mybir.AluOpType
AX = mybir.AxisListType

N_CORES = 8
SEQ = 2048
DM = 1024
NSEQ_CORE = 4
GT = 512
NG_SEQ = SEQ // GT
NGT_FULL = NSEQ_CORE * NG_SEQ
NCHUNK = 24
EPS = 1e-6
NEG = -30000.0


class _Op:
    __slots__ = ("eng", "fn", "deps", "sig", "sigval", "dsem", "dval", "idx")

    def __init__(self, eng, fn, dsem=None):
        self.eng = eng
        self.fn = fn
        self.deps = []
        self.sig = False
        self.sigval = (0, 0)
        self.dsem = dsem
        self.dval = 0
        self.idx = 0


class Sched:
    ENGS = ("pe", "act", "dve", "pool", "sp")

    def __init__(self):
        self.ops = {e: [] for e in self.ENGS}
        self.last_w = {}
        self.readers = {}
        self.dsem_count = {}
        self.all_ops = []

    def op(self, eng, fn, reads=(), writes=(), dsem=None):
        o = _Op(eng, fn, dsem)
        o.idx = len(self.all_ops)
        self.all_ops.append(o)
        deps = {}
        for k in reads:
            w = self.last_w.get(k)
            if w is not None:
                deps[w.idx] = w
            if isinstance(k, tuple) and isinstance(k[0], str) and k[0].startswith("ps"):
                for sk, r in self.readers.get(k, {}).items():
                    if sk != eng:
                        deps[r.idx] = r
        for k in writes:
            w = self.last_w.get(k)
            if w is not None:
                deps[w.idx] = w
            for r in self.readers.get(k, {}).values():
                deps[r.idx] = r
        for d in deps.values():
            if d is o:
                continue
            if d.eng == eng and d.dsem is None and dsem is None:
                israw = any(self.last_w.get(k) is d for k in reads)
                if not israw or eng == "pe":
                    continue
            o.deps.append(d)
        for k in writes:
            self.last_w[k] = o
            self.readers[k] = {}
        for k in reads:
            if k in writes:
                continue
            sk = dsem if dsem is not None else eng
            self.readers.setdefault(k, {})[sk] = o
        if dsem is not None:
            self.dsem_count[dsem] = self.dsem_count.get(dsem, 0) + 16
            o.dval = self.dsem_count[dsem]
        self.ops[eng].append(o)
        return o

    EPOCH = 3000

    def finalize(self):
        for o in self.all_ops:
            for d in o.deps:
                if d.dsem is None:
                    d.sig = True
        self.nepoch = {}
        for e in self.ENGS:
            c = 0
            for o in self.ops[e]:
                if o.dsem is None and o.sig:
                    o.sigval = (c // self.EPOCH, c % self.EPOCH + 1)
                    c += 1
            self.nepoch[e] = max(1, (c + self.EPOCH - 1) // self.EPOCH)

    def emit(self, eng, handle, sems, dsems, final_waits=()):
        waited = {}
        for o in self.ops[eng]:
            need = {}
            for d in o.deps:
                if d.dsem is not None:
                    key, val = ("d", d.dsem), d.dval
                else:
                    key, val = ("e", d.eng, d.sigval[0]), d.sigval[1]
                    if any(k[0] == "e" and k[1] == d.eng and k[2] > d.sigval[0] for k in waited):
                        continue
                if waited.get(key, 0) >= val:
                    continue
                if need.get(key, 0) < val:
                    need[key] = val
            for key, val in need.items():
                s = dsems[key[1]] if key[0] == "d" else sems[key[1]][key[2]]
                handle.wait_ge(s, val)
                waited[key] = val
            ins = o.fn(handle)
            if o.dsem is not None:
                ins.then_inc(dsems[o.dsem], 16)
            elif o.sig:
                ins.then_inc(sems[eng][o.sigval[0]], 1)
        for s, v in final_waits:
            handle.wait_ge(s, v)


def build(NGT=NGT_FULL, chunk_seq_in=None):
    dry = chunk_seq_in is None
    nc = bass.Bass("TRN2", target_bir_lowering=False)
    x_d = nc.dram_tensor("x", [NSEQ_CORE * SEQ, DM], F32, kind="ExternalInput").ap()
    wall_d = nc.dram_tensor("wall", [NCHUNK, 128, 4096], F32, kind="ExternalInput").ap()
    gbc_d = nc.dram_tensor("gbc", [128, 3 * DM], F32, kind="ExternalInput").ap()
    rope_d = nc.dram_tensor("rope", [128, 16 * 32], F32, kind="ExternalInput").ap()
    rmask_d = nc.dram_tensor("rmask", [128, 6], F32, kind="ExternalInput").ap()
    ident_d = nc.dram_tensor("ident", [128, 128], BF16, kind="ExternalInput").ap()
    bias_d = nc.dram_tensor("biasg", [128, 8 * 5 * 128], F32, kind="ExternalInput").ap()
    lamv_d = nc.dram_tensor("lamv", [128, 128], F32, kind="ExternalInput").ap()
    gsub_d = nc.dram_tensor("gsub", [128, 64], F32, kind="ExternalInput").ap()
    out_d = nc.dram_tensor("out", [NSEQ_CORE * SEQ, DM], F32, kind="ExternalOutput").ap()
    wscr = nc.dram_tensor("wscr", [NCHUNK, 128, 4096], BF16).ap()

    S = Sched()
    cap_stack = []

    def OP(eng, fn, reads=(), writes=(), dsem=None):
        if dry and not cap_stack:
            return
        if cap_stack:
            cap_stack[-1].append((eng, fn, list(reads), list(writes), dsem))
        else:
            S.op(eng, fn, reads=reads, writes=writes, dsem=dsem)

    @contextlib.contextmanager
    def cap():
        lst = []
        cap_stack.append(lst)
        try:
            yield lst
        finally:
            cap_stack.pop()

    def replay(lst):
        if dry:
            return
        for (eng, fn, reads, writes, dsem) in lst:
            S.op(eng, fn, reads=reads, writes=writes, dsem=dsem)

    A = nc.alloc_sbuf_tensor
    xg = [A(f"xg{i}", [128, 4, DM], F32) for i in range(2)]
    actT = A("actT", [128, 8, GT], BF16)
    h2T = A("h2T", [128, 8, GT], BF16)
    hbf = [A(f"hbf{i}", [128, DM], BF16) for i in range(2)]
    junk = A("junk", [128, DM], BF16)
    W = [A(f"W{i}", [128, 8, 512], BF16) for i in range(3)]
    kaT = A("kaT", [128, 4, SEQ], BF16)
    va = A("va", [128, 16, 8, 65], BF16)
    kbT = A("kbT", [128, 4, 1024], BF16)
    vb = A("vb", [128, 8, 8, 65], BF16)
    qaZ = A("qaZ", [128, 4, 4, GT], BF16)
    qbZ = A("qbZ", [128, 4, 2, GT], BF16)
    rmask = A("rmask_sb", [128, 6], F32)
    ropeo = [A(f"ropeo{i}", [128, 512], BF16) for i in range(2)]
    rt = [A(f"rt{i}", [128, 256], F32) for i in range(4)]
    PT = [A(f"PT{i}", [128, 512], BF16) for i in range(3)]
    obuf = [A(f"obuf{i}", [128, 8, 64], F32) for i in range(2)]
    sqo = A("sqo", [128, 8, 64], F32)
    mix = A("mix", [128, 2, DM], BF16)
    aT = [A(f"aT{i}", [128, 8, GT], BF16) for i in range(1)]
    sq = [A(f"sq{i}", [128, 512], F32) for i in range(2)]
    gbc12 = A("gbc12_sb", [128, 2, DM], BF16)
    gbcf = A("gbcf_sb", [128, DM], F32)
    ropec = A("ropec", [128, 16, 2, 16], F32)
    ident = A("ident_sb", [128, 128], BF16)
    biasT = A("biasT", [128, 8, 640], BF16)
    lamv = A("lamv_sb", [128, 4, 32], F32)
    lamt = A("lamt", [128, 2, 32], F32)
    lams = A("lams", [128, 8], F32)
    gsub = A("gsub_sb", [128, 64], F32)
    st = A("st", [128, 3, 4, 4], F32)
    rec = [A(f"rec{i}", [128, 2], F32) for i in range(2)]
    t1 = [A(f"t1_{i}", [128, 64], F32) for i in range(2)]
    recb = [A(f"recb{i}", [128, 1], F32) for i in range(2)]
    so = A("so", [128, 3, 8], F32)
    maskc = A("maskc", [128, 1], F32)

    PS = [nc.alloc_psum_tensor(f"psf{i}", [128, 512], F32) for i in (0, 1)]
    PS_TP = [nc.alloc_psum_tensor(f"pstp{i}", [128, 1024], BF16) for i in (0, 1)]
    PS_S = [nc.alloc_psum_tensor(f"pss{i}", [128, 512], F32) for i in (0, 1)]
    PS_ACC = [nc.alloc_psum_tensor(f"psa{i}", [128, 512], F32) for i in (0, 1)]

    cnt = {}

    def rot(name, n):
        v = cnt.get(name, 0) % n
        cnt[name] = cnt.get(name, 0) + 1
        return v

    def I(name, *args, **kwargs):
        return lambda e: getattr(e, name)(*args, **kwargs)

    OP("sp", I("dma_start", out=ident[:], in_=ident_d), writes=["ident"], dsem="dc0")
    OP("pool", I("dma_start", out=gbc12[:], in_=gbc_d[:, 0:2 * DM].rearrange("p (a d) -> p a d", a=2)),
       writes=["gbc12"], dsem="dc1")
    OP("sp", I("dma_start", out=gbcf[:], in_=gbc_d[:, 2 * DM:3 * DM]), writes=["gbcf"], dsem="dc6")
    OP("sp", I("dma_start", out=rmask[:], in_=rmask_d), writes=["rmask"], dsem="dc7")
    OP("sp", I("dma_start", out=ropec[:].rearrange("p a b c -> p (a b c)"), in_=rope_d), writes=["ropec"],
         dsem="dc2")
    OP("sp", I("dma_start", out=lamv[:].rearrange("p a d -> p (a d)"), in_=lamv_d), writes=["lamv"], dsem="dc3")
    OP("sp", I("dma_start", out=gsub[:], in_=gsub_d), writes=["gsub"], dsem="dc4")
    OP("pool", I("dma_start", out=biasT[:], in_=bias_d.rearrange("p (h q) -> p h q", h=8)), writes=["biasT"],
         dsem="dc5")
    OP("pool", I("memset", maskc[0:64, :], 0.0), writes=["maskc_lo"])
    OP("pool", I("memset", maskc[64:128, :], NEG), writes=["maskc"])
    OP("dve", I("tensor_scalar", out=biasT[:], in0=biasT[:], scalar1=8.0, scalar2=None, op0=ALU.mult),
       reads=["biasT"], writes=["bias8"])
    OP("pool", I("memset", va[:].rearrange("p a h e -> p (a h) e")[:, :, 64:65], 1.0), writes=["va_ones"])
    OP("pool", I("memset", vb[:].rearrange("p a h e -> p (a h) e")[:, :, 64:65], 1.0), writes=["vb_ones"])
    OP("dve", I("tensor_tensor", out=lamt[:, 0, :], in0=lamv[:, 0, :], in1=lamv[:, 1, :], op=ALU.mult),
         reads=["lamv"], writes=["lamt0"])
    OP("dve", I("tensor_tensor", out=lamt[:, 1, :], in0=lamv[:, 2, :], in1=lamv[:, 3, :], op=ALU.mult),
         reads=["lamv"], writes=["lamt1"])
    OP("dve", I("tensor_reduce", out=lams[:, 0:2], in_=lamt[:], axis=AX.X, op=ALU.add),
         reads=["lamt0", "lamt1"], writes=["lams01"])
    OP("act", I("activation", out=lams[:, 2:4], in_=lams[:, 0:2], func=AF.Exp), reads=["lams01"],
         writes=["lams23"])
    OP("dve", I("scalar_tensor_tensor", out=lams[:, 4:5], in0=lams[:, 3:4], scalar=-0.2, in1=lams[:, 2:3],
                  op0=ALU.add, op1=ALU.subtract), reads=["lams23"], writes=["neglam"])
    OP("dve", I("tensor_scalar", out=gsub[:], in0=gsub[:], scalar1=0.8, scalar2=None, op0=ALU.mult),
         reads=["gsub"], writes=["gsub"])
    neglam = lams[:, 4:5]

    for c in range(NCHUNK):
        OP("pool", I("dma_start", out=wscr[c].rearrange("p (a n) -> p a n", a=8),
                       in_=wall_d[c].rearrange("p (a n) -> p a n", a=8)),
             writes=[("wscr", c)], dsem=f"dw{c}")

    wstate = {"next": 0, "cons": 0}
    chunk_seq = [] if dry else list(chunk_seq_in)

    def issue_wload(n):
        c = chunk_seq[n]
        slot = n % 3
        OP("sp", I("dma_start", out=W[slot][:].rearrange("p a n -> p (a n)"), in_=wscr[c]),
           reads=[("wscr", c)], writes=[("W", slot)], dsem=f"dW{slot}")

    def get_chunk(c):
        n = wstate["cons"]
        wstate["cons"] += 1
        if dry:
            chunk_seq.append(c)
            return n % 3
        assert chunk_seq[n] == c, (n, chunk_seq[n], c)
        assert not cap_stack
        while wstate["next"] <= min(n + 2, len(chunk_seq) - 1):
            issue_wload(wstate["next"])
            wstate["next"] += 1
        return n % 3

    def issue_xload(G):
        slot = G % 2
        src = x_d[G * GT:(G + 1) * GT, :].rearrange("(t p) d -> p t d", p=128)
        OP("sp", I("dma_start", out=xg[slot][:], in_=src),
             writes=[("xg", slot, t) for t in range(4)], dsem=f"dx{slot}")

    def transpose_masked(src, src_keys, dstZ, ngrp, mcol0, t, dkey):
        b = rot("tp", 2)
        for i in range(4):
            OP("pe", I("transpose", out=PS_TP[b][:, i * 128:(i + 1) * 128], in_=src[:, i * 128:(i + 1) * 128],
                       identity=ident[:]), reads=list(src_keys) + ["ident"], writes=[("pstp", b)])
        src_ps = PS_TP[b][:, 0:512].rearrange("p (a b) -> p a b", a=4)
        use_dve = rot("tm", 2) == 0
        for r in range(ngrp):
            dst = dstZ[:, :, r, t * 128:(t + 1) * 128]
            if use_dve:
                OP("dve", I("tensor_scalar", out=dst, in0=src_ps, scalar1=rmask[:, mcol0 + r:mcol0 + r + 1],
                            scalar2=None, op0=ALU.mult), reads=[("pstp", b), "rmask"], writes=[(dkey, t, r)])
            else:
                OP("act", I("mul", out=dst, in_=src_ps, mul=rmask[:, mcol0 + r:mcol0 + r + 1]),
                   reads=[("pstp", b), "rmask"], writes=[(dkey, t, r)])

    def transpose_to(src, src_keys, nblk, dst_ap, dst_keys, eng):
        b = rot("tp", 2)
        for i in range(nblk):
            OP("pe", I("transpose", out=PS_TP[b][:, i * 128:(i + 1) * 128], in_=src[:, i * 128:(i + 1) * 128],
                         identity=ident[:]), reads=list(src_keys) + ["ident"], writes=[("pstp", b)])
        src_ps = PS_TP[b][:, 0:nblk * 128].rearrange("p (a b) -> p a b", a=nblk)
        if eng == "act":
            OP("act", I("activation", out=dst_ap, in_=src_ps, func=AF.Copy), reads=[("pstp", b)], writes=dst_keys)
        else:
            OP("dve", I("tensor_copy", out=dst_ap, in_=src_ps), reads=[("pstp", b)], writes=dst_keys)

    def rms_stats(xs_t, xkey, kind, t, hs=None):
        OP("act", I("activation", out=junk[:], in_=xs_t, func=AF.Square, accum_out=st[:, kind, t, 0:1]),
             reads=[xkey], writes=["junk", ("st", kind, t, 0)])
        OP("act", I("activation", out=st[:, kind, t, 1:2], in_=st[:, kind, t, 0:1], func=AF.Ln,
                      scale=1.0 / DM, bias=EPS), reads=[("st", kind, t, 0)], writes=[("st", kind, t, 1)])
        OP("act", I("activation", out=st[:, kind, t, 2:3], in_=st[:, kind, t, 1:2], func=AF.Exp, scale=-0.5),
             reads=[("st", kind, t, 1)], writes=[("st", kind, t, 2)])

    def norm_to_actT(G, kind, t, dstT=None, dkey="actT"):
        dstT = actT if dstT is None else dstT
        xs_t = xg[G % 2][:, t, :]
        xkey = ("xg", G % 2, t)
        hs = rot("hbf", 2)
        rms_stats(xs_t, xkey, kind, t, hs)
        OP("dve", I("scalar_tensor_tensor", out=hbf[hs][:], in0=xs_t, scalar=st[:, kind, t, 2:3],
                      in1=gbc12[:, kind, :], op0=ALU.mult, op1=ALU.mult),
             reads=[xkey, ("st", kind, t, 2), "gbc12"], writes=[("hbf", hs)])
        transpose_to(hbf[hs], [("hbf", hs)], 8, dstT[:, :, t * 128:(t + 1) * 128], [(dkey, t)], "act")

    def norm_phase(G, kind, dstT, dkey, filler=None, nfill=0):
        xs = xg[G % 2]
        for t in range(4):
            rms_stats(xs[:, t, :], ("xg", G % 2, t), kind, t)
        pend = None
        for t in range(4):
            hs = rot("hbf", 2)
            OP("dve", I("scalar_tensor_tensor", out=hbf[hs][:], in0=xs[:, t, :], scalar=st[:, kind, t, 2:3],
                        in1=gbc12[:, kind, :], op0=ALU.mult, op1=ALU.mult),
               reads=[("xg", G % 2, t), ("st", kind, t, 2), "gbc12"], writes=[("hbf", hs)])
            if pend is not None:
                replay(pend)
            b = rot("tp", 2)
            for i in range(8):
                OP("pe", I("transpose", out=PS_TP[b][:, i * 128:(i + 1) * 128], in_=hbf[hs][:, i * 128:(i + 1) * 128],
                           identity=ident[:]), reads=[("hbf", hs), "ident"], writes=[("pstp", b)])
            with cap() as pend:
                OP("dve", I("tensor_copy", out=dstT[:, :, t * 128:(t + 1) * 128],
                            in_=PS_TP[b][:, :].rearrange("p (a b) -> p a b", a=8)),
                   reads=[("pstp", b)], writes=[(dkey, t)])
            if filler is not None:
                for _ in range(nfill):
                    next(filler, None)
        replay(pend)

    def dense_tokmajor(wslot, lhs_list, lhs_keys):
        b = rot("mm", 2)
        nk = len(lhs_list)
        for k in range(nk):
            OP("pe", I("matmul", PS[b][:, :], lhsT=lhs_list[k], rhs=W[wslot][:, k, :], start=(k == 0),
                         stop=(k == nk - 1)), reads=list(lhs_keys) + [("W", wslot)], writes=[("psf", b)])
        return b

    def residual_add(G, b, t, dh):
        xs = xg[G % 2]
        OP("dve", I("tensor_tensor", out=xs[:, t, dh * 512:(dh + 1) * 512], in0=PS[b][:, :],
                      in1=xs[:, t, dh * 512:(dh + 1) * 512], op=ALU.add),
             reads=[("psf", b), ("xg", G % 2, t)], writes=[("xg", G % 2, t)])

    def inproj(G, cg, ws, t):
        gs = G % NG_SEQ
        jt = gs * 4 + t
        b = dense_tokmajor(ws, [actT[:, k, t * 128:(t + 1) * 128] for k in range(8)], [("actT", t)])
        ps = PS[b]
        if cg in (0, 1):
            v = ps[:, :].rearrange("p (s h j) -> p s h j", s=16, h=2, j=16)
            x1, x2 = v[:, :, 0, :], v[:, :, 1, :]
            cosb = ropec[:, jt, 0, :].unsqueeze(1).to_broadcast([128, 16, 16])
            sinb = ropec[:, jt, 1, :].unsqueeze(1).to_broadcast([128, 16, 16])
            r3 = [r[:].rearrange("p (s j) -> p s j", s=16) for r in rt]
            for i, (a_, b_) in enumerate([(x1, cosb), (x2, sinb), (x2, cosb), (x1, sinb)]):
                OP("dve", I("tensor_tensor", out=r3[i], in0=a_, in1=b_, op=ALU.mult),
                     reads=[("psf", b), "ropec"], writes=[f"rt{i}"])
            ro = rot("ropeo", 2)
            rov = ropeo[ro][:].rearrange("p (s h j) -> p s h j", s=16, h=2, j=16)
            OP("pool", I("tensor_tensor", out=rov[:, :, 0, :], in0=r3[0], in1=r3[1], op=ALU.subtract),
                 reads=["rt0", "rt1"], writes=[("ropeo", ro, 0)])
            OP("pool", I("tensor_tensor", out=rov[:, :, 1, :], in0=r3[2], in1=r3[3], op=ALU.add),
                 reads=["rt2", "rt3"], writes=[("ropeo", ro, 1)])
            with cap() as deferred:
                if cg == 0:
                    transpose_masked(ropeo[ro], [("ropeo", ro, 0), ("ropeo", ro, 1)], qaZ, 4, 0, t, "qaZ")
                else:
                    dst, dk = kaT[:, :, jt * 128:(jt + 1) * 128], [("kaT", jt)]
                    transpose_to(ropeo[ro], [("ropeo", ro, 0), ("ropeo", ro, 1)], 4, dst, dk, "dve")
            return deferred
        elif cg in (3, 4):
            ro = rot("ropeo", 2)
            OP("act", I("activation", out=ropeo[ro][:], in_=ps[:, :], func=AF.Copy),
                 reads=[("psf", b)], writes=[("ropeo", ro, 0), ("ropeo", ro, 1)])
            with cap() as deferred:
                if cg == 3:
                    transpose_masked(ropeo[ro], [("ropeo", ro, 0), ("ropeo", ro, 1)], qbZ, 2, 4, t, "qbZ")
                else:
                    sl = jt % 8
                    dst, dk = kbT[:, :, sl * 128:(sl + 1) * 128], [("kbT", sl)]
                    transpose_to(ropeo[ro], [("ropeo", ro, 0), ("ropeo", ro, 1)], 4, dst, dk, "dve")
            return deferred
        else:
            pv = ps[:, :].rearrange("p (h d) -> p h d", h=8)
            if cg == 2:
                dst, dk = va[:, jt, :, 0:64], [("va", jt)]
            else:
                sl = jt % 8
                dst, dk = vb[:, sl, :, 0:64], [("vb", sl)]
            OP("act", I("activation", out=dst, in_=pv, func=AF.Copy), reads=[("psf", b)], writes=dk)
            return []

    def attn_units(G):
        units = []
        for t in range(4):
            qb = (G % NG_SEQ) * 4 + t
            ob = rot("ob", 2)
            jlo = max(0, qb - 4)

            def a_head(h, t=t, qb=qb, ob=ob, jlo=jlo):
                ab = rot("acc", 2)
                njt = qb + 1
                for c in range(2):
                    s = 2 * h + c
                    rg, ti = s % 4, s // 4
                    for j0 in range(0, njt, 4):
                        n = min(4, njt - j0)
                        sb = rot("s", 2)
                        pt = rot("pt", 3)
                        u = {}
                        with cap() as u["S"]:
                            for jj in range(n):
                                j = j0 + jj
                                OP("pe", I("matmul", PS_S[sb][:, jj * 128:(jj + 1) * 128],
                                           lhsT=kaT[:, ti, j * 128:(j + 1) * 128],
                                           rhs=qaZ[:, ti, rg, t * 128:(t + 1) * 128], start=True, stop=True),
                                   reads=[("kaT", j), ("qaZ", t, rg)], writes=[("pss", sb)])
                        with cap() as u["post"]:
                            OP("act", I("activation", out=PT[pt][:, 0:n * 128], in_=PS_S[sb][:, 0:n * 128],
                                        func=AF.Exp, scale=32 ** -0.5), reads=[("pss", sb)], writes=[("PT", pt)])
                            if j0 + n == njt:
                                c0 = (n - 1) * 128
                                OP("act", I("activation", out=PT[pt][:, c0:c0 + 64], in_=PS_S[sb][:, c0:c0 + 64],
                                            func=AF.Exp, scale=32 ** -0.5, bias=maskc[:, 0:1]),
                                   reads=[("pss", sb), "maskc"], writes=[("PT", pt)])
                        with cap() as u["PV"]:
                            for jj in range(n):
                                j = j0 + jj
                                OP("pe", I("matmul", PS_ACC[ab][:, c * 65:(c + 1) * 65],
                                           lhsT=PT[pt][:, jj * 128:(jj + 1) * 128], rhs=va[:, j, h, :],
                                           start=(j == 0), stop=(j == njt - 1)),
                                   reads=[("PT", pt), ("va", j), "va_ones"], writes=[("psa", ab)])
                        with cap() as u["fin"]:
                            if c == 1 and j0 + n == njt:
                                rc = rot("rec", 2)
                                accv = PS_ACC[ab][:, 0:130].rearrange("p (c e) -> p c e", c=2)
                                OP("dve", I("reciprocal", out=rec[rc][:], in_=accv[:, :, 64]), reads=[("psa", ab)],
                                   writes=[("rec", rc)])
                                OP("dve", I("tensor_scalar", out=t1[rc][:], in0=PS_ACC[ab][:, 65:129],
                                            scalar1=rec[rc][:, 1:2], scalar2=neglam, op0=ALU.mult, op1=ALU.mult),
                                   reads=[("psa", ab), ("rec", rc), "neglam"], writes=[("t1", rc)])
                                OP("dve", I("scalar_tensor_tensor", out=obuf[ob][:, h, :], in0=PS_ACC[ab][:, 0:64],
                                            scalar=rec[rc][:, 0:1], in1=t1[rc][:], op0=ALU.mult, op1=ALU.add),
                                   reads=[("psa", ab), ("rec", rc), ("t1", rc)], writes=[("obuf", ob)])
                                if h == 7:
                                    OP("pool", I("tensor_tensor", out=sqo[:], in0=obuf[ob][:], in1=obuf[ob][:],
                                                 op=ALU.mult), reads=[("obuf", ob)], writes=["sqo"])
                                    OP("dve", I("tensor_reduce", out=so[:, 0, :], in_=sqo[:], axis=AX.X, op=ALU.add),
                                       reads=["sqo"], writes=["so0"])
                                    OP("act", I("activation", out=so[:, 1, :], in_=so[:, 0, :], func=AF.Ln,
                                                scale=1.0 / 64, bias=EPS), reads=["so0"], writes=["so1"])
                                    OP("act", I("activation", out=so[:, 2, :], in_=so[:, 1, :], func=AF.Exp,
                                                scale=-0.5), reads=["so1"], writes=["so2"])
                                    OP("pool", I("tensor_tensor", out=sqo[:], in0=obuf[ob][:],
                                                 in1=so[:, 2, :].unsqueeze(2).to_broadcast([128, 8, 64]), op=ALU.mult),
                                       reads=[("obuf", ob), "so2"], writes=["sqo"])
                                    OP("pool", I("tensor_tensor",
                                                 out=mix[:, t % 2, 0:512].rearrange("p (h d) -> p h d", h=8), in0=sqo[:],
                                                 in1=gsub[:].unsqueeze(1).to_broadcast([128, 8, 64]), op=ALU.mult),
                                       reads=["sqo", "gsub"], writes=[("mixA", t % 2)])
                        units.append(u)

            def b_head(h, t=t, qb=qb, ob=ob, jlo=jlo):
                ab = rot("acc", 2)
                rg, ti = h % 2, h // 2
                njt = qb - jlo + 1
                for j0 in range(0, njt, 4):
                    n = min(4, njt - j0)
                    sb = rot("s", 2)
                    pt = rot("pt", 3)
                    u = {}
                    with cap() as u["S"]:
                        jb = (jlo + j0) - qb + 4
                        OP("pe", I("matmul", PS_S[sb][:, 0:n * 128], lhsT=ident[:],
                                   rhs=biasT[:, h, jb * 128:(jb + n) * 128], start=True, stop=False),
                           reads=["ident", "bias8"], writes=[("pss", sb)])
                        for jj in range(n):
                            sl = (jlo + j0 + jj) % 8
                            OP("pe", I("matmul", PS_S[sb][:, jj * 128:(jj + 1) * 128],
                                       lhsT=kbT[:, ti, sl * 128:(sl + 1) * 128],
                                       rhs=qbZ[:, ti, rg, t * 128:(t + 1) * 128], start=False, stop=(jj == n - 1)),
                               reads=[("kbT", sl), ("qbZ", t, rg)], writes=[("pss", sb)])
                    with cap() as u["post"]:
                        OP("act", I("activation", out=PT[pt][:, 0:n * 128], in_=PS_S[sb][:, 0:n * 128], func=AF.Exp,
                                    scale=0.125), reads=[("pss", sb)], writes=[("PT", pt)])
                    with cap() as u["PV"]:
                        for jj in range(n):
                            j = jlo + j0 + jj
                            sl = j % 8
                            OP("pe", I("matmul", PS_ACC[ab][:, 0:65], lhsT=PT[pt][:, jj * 128:(jj + 1) * 128],
                                       rhs=vb[:, sl, h, :], start=(j == jlo), stop=(j == qb)),
                               reads=[("PT", pt), ("vb", sl), "vb_ones"], writes=[("psa", ab)])
                    with cap() as u["fin"]:
                        if j0 + n == njt:
                            rb = rot("recb", 2)
                            OP("dve", I("reciprocal", out=recb[rb][:], in_=PS_ACC[ab][:, 64:65]),
                               reads=[("psa", ab)], writes=[("recb", rb)])
                            OP("dve", I("tensor_scalar", out=mix[:, t % 2, 512 + h * 64:512 + (h + 1) * 64],
                                        in0=PS_ACC[ab][:, 0:64], scalar1=recb[rb][:, 0:1], scalar2=None,
                                        op0=ALU.mult), reads=[("psa", ab), ("recb", rb)], writes=[("mixB", t % 2)])
                    if h == 7 and j0 + n == njt:
                        with cap() as u["late"]:
                            transpose_to(mix[:, t % 2, :], [("mixA", t % 2), ("mixB", t % 2)], 8,
                                         actT[:, :, t * 128:(t + 1) * 128], [("actT", t)], "act")
                    units.append(u)

            for h in range(8):
                a_head(h)
                b_head(h)
        return units

    def ffn_gen(Gp):
        order = [(1, 0, 8), (2, 0, 10), (1, 1, 12), (2, 1, 14), (1, 2, 16), (2, 2, 18), (1, 3, 20), (2, 3, 22)]
        for kind, r, cidx in order:
            if kind == 1:
                for half in range(2):
                    ws = get_chunk(cidx + half)
                    for f4 in range(4):
                        f8 = half * 4 + f4
                        b = rot("mm", 2)
                        for k in range(8):
                            OP("pe", I("matmul", PS[b][:, :], lhsT=W[ws][:, k, f4 * 128:(f4 + 1) * 128],
                                       rhs=h2T[:, k, :], start=(k == 0), stop=(k == 7)),
                               reads=[("W", ws)] + [("h2T", tt) for tt in range(4)], writes=[("psf", b)])
                            if k == 7:
                                sqs = rot("sq", 2)
                                OP("dve", I("tensor_scalar", out=sq[sqs][:], in0=PS[b][:, :], scalar1=0.0,
                                            scalar2=None, op0=ALU.max), reads=[("psf", b)], writes=[("sq", sqs)])
                                OP("pool", I("tensor_tensor", out=aT[0][:, f8, :], in0=sq[sqs][:],
                                             in1=sq[sqs][:], op=ALU.mult), reads=[("sq", sqs)],
                                   writes=[("aT", 0, f8)])
                            yield
            else:
                for dh in range(2):
                    ws = get_chunk(cidx + dh)
                    for t in range(4):
                        b = rot("mm", 2)
                        for k in range(8):
                            OP("pe", I("matmul", PS[b][:, :], lhsT=aT[0][:, k, t * 128:(t + 1) * 128],
                                       rhs=W[ws][:, k, :], start=(k == 0), stop=(k == 7)),
                               reads=[("aT", 0, k), ("W", ws)], writes=[("psf", b)])
                            if k == 7:
                                residual_add(Gp, b, t, dh)
                            yield
        final_norm_store(Gp)
        if Gp + 2 < NGT:
            issue_xload(Gp + 2)
        yield

    def final_norm_store(G):
        xs = xg[G % 2]
        for t in range(4):
            rms_stats(xs[:, t, :], ("xg", G % 2, t), 2, t)
        for t in range(4):
            OP("dve", I("scalar_tensor_tensor", out=xs[:, t, :], in0=xs[:, t, :], scalar=st[:, 2, t, 2:3],
                        in1=gbcf[:], op0=ALU.mult, op1=ALU.mult),
               reads=[("xg", G % 2, t), ("st", 2, t, 2), "gbcf"], writes=[("xg", G % 2, t)])
        dst = out_d[G * GT:(G + 1) * GT, :].rearrange("(t p) d -> p t d", p=128)
        OP("pool", I("dma_start", out=dst, in_=xs[:]), reads=[("xg", G % 2, t) for t in range(4)],
           dsem=f"do{G % 2}")

    issue_xload(0)
    if NGT > 1:
        issue_xload(1)
    N_DENSE = 513
    N_FILL = 8
    norm_phase(0, 0, actT, "actT")
    dense = iter(())
    n_left = 0
    for G in range(NGT):
        pending = []
        for cg in range(6):
            ws = get_chunk(cg)
            for t in range(4):
                nxt = inproj(G, cg, ws, t)
                replay(pending)
                pending = nxt
        replay(pending)
        units = attn_units(G)
        n = len(units)
        replay(units[0]["S"])
        accf = 0.0
        rate = n_left * 1.0 / n
        LATE = 10
        for ui in range(n):
            if ui + 1 < n:
                replay(units[ui + 1]["S"])
            replay(units[ui]["post"])
            accf += rate
            while accf >= 1.0:
                accf -= 1.0
                next(dense, None)
            replay(units[ui]["PV"])
            replay(units[ui]["fin"])
            if ui >= LATE and "late" in units[ui - LATE]:
                replay(units[ui - LATE]["late"])
        for ui in range(max(0, n - LATE), n):
            if "late" in units[ui]:
                replay(units[ui]["late"])
        for _ in dense:
            pass
        for dh in range(2):
            ws = get_chunk(6 + dh)
            for t in range(4):
                b = dense_tokmajor(ws, [actT[:, k, t * 128:(t + 1) * 128] for k in range(8)], [("actT", t)])
                residual_add(G, b, t, dh)
        norm_phase(G, 1, h2T, "h2T")
        dense = ffn_gen(G)
        n_left = N_DENSE
        if G + 1 < NGT:
            norm_phase(G + 1, 0, actT, "actT", dense, N_FILL)
            n_left -= 4 * N_FILL
    for _ in dense:
        pass
    if dry:
        return chunk_seq

    S.finalize()
    dsem_names = sorted(S.dsem_count.keys())
    with contextlib.ExitStack() as es:
        sems = {e: [es.enter_context(nc.semaphore(f"s_{e}{i}")) for i in range(S.nepoch[e])]
                for e in Sched.ENGS}
        dsems = {k: es.enter_context(nc.semaphore(f"ds_{k}")) for k in dsem_names}
        block = es.enter_context(nc.Block())
        finals = [(dsems[k], S.dsem_count[k]) for k in ("do0", "do1") if k in S.dsem_count]

        @block.sync
        def _(eng):
            S.emit("sp", eng, sems, dsems)

        @block.scalar
        def _(eng):
            S.emit("act", eng, sems, dsems)

        @block.vector
        def _(eng):
            S.emit("dve", eng, sems, dsems)

        @block.gpsimd
        def _(eng):
            S.emit("pool", eng, sems, dsems, final_waits=finals)

        @block.tensor
        def _(eng):
            S.emit("pe", eng, sems, dsems)
    return nc


def _chunk_k(w, c0):
    return np.ascontiguousarray(w[:, c0:c0 + 512].reshape(8, 128, 512).transpose(1, 0, 2)).reshape(128, 4096)


def _prep_shared(w_in, w_out, norm1_g, norm2_g, final_g, subln_g, lambda_q1, lambda_k1, lambda_q2, lambda_k2,
                 rel_bias, w_ff1, w_ff2):
    w_in, w_out, w_ff1, w_ff2 = (np.asarray(a, np.float32)[0] for a in (w_in, w_out, w_ff1, w_ff2))
    chunks = [_chunk_k(w_in, c * 512) for c in range(6)] + [_chunk_k(w_out, c * 512) for c in range(2)]

    def f1(r):
        return [_chunk_k(w_ff1, r * 1024 + hh * 512) for hh in range(2)]

    def f2(r):
        return [_chunk_k(w_ff2[r * 1024:(r + 1) * 1024], dh * 512) for dh in range(2)]

    chunks += f1(0) + f2(0) + f1(1) + f2(1) + f1(2) + f2(2) + f1(3) + f2(3)
    wall = np.stack(chunks, 0)
    gbc = np.concatenate([np.broadcast_to(np.asarray(g, np.float32).reshape(1, DM), (128, DM))
                          for g in (norm1_g, norm2_g, final_g)], axis=1)
    half = 16
    inv_freq = (np.float32(10000.0) ** (-np.arange(half, dtype=np.float32) / np.float32(half))).astype(np.float32)
    pos = np.arange(SEQ, dtype=np.float32)
    ang = pos[:, None] * inv_freq[None, :]
    cs = np.stack([np.cos(ang), np.sin(ang)], axis=1).astype(np.float32)
    rope = np.ascontiguousarray(cs.reshape(16, 128, 2, 16).transpose(1, 0, 2, 3)).reshape(128, 512)
    ident = np.eye(128, dtype=np.float32).astype(ml_dtypes.bfloat16)
    tab = np.concatenate([np.asarray(rel_bias, np.float32)[0], np.full((8, 1), NEG, np.float32)], axis=1)
    k = np.arange(128)[:, None, None]
    jb = np.arange(5)[None, :, None]
    q = np.arange(128)[None, None, :]
    r = 4 - jb
    idx = np.clip(r * 128 + q - k, -256, 256) + 256
    masked = ((r == 0) & (q < 64) & (k >= 64)) | ((r == 4) & (q >= 64) & (k < 64))
    idx = np.where(masked, 513, idx)
    biasg = np.ascontiguousarray(tab[:, idx].transpose(1, 0, 2, 3)).reshape(128, 8 * 5 * 128)
    lamv = np.concatenate([np.broadcast_to(np.asarray(v, np.float32).reshape(1, 32), (128, 32))
                           for v in (lambda_q1, lambda_k1, lambda_q2, lambda_k2)], axis=1)
    gsub = np.ascontiguousarray(np.broadcast_to(np.asarray(subln_g, np.float32).reshape(1, 64), (128, 64)))
    rmask = np.zeros((128, 6), np.float32)
    for r in range(4):
        rmask[32 * r:32 * r + 32, r] = 1.0
    for r in range(2):
        rmask[64 * r:64 * r + 64, 4 + r] = 1.0
    return {"rmask": rmask, "wall": wall, "gbc": np.ascontiguousarray(gbc), "rope": rope, "ident": ident, "biasg": biasg,
            "lamv": np.ascontiguousarray(lamv), "gsub": gsub}


_NC_CACHE = {}


def kernel(x, w_in, w_out, norm1_g, norm2_g, final_g, subln_g, lambda_q1, lambda_k1, lambda_q2, lambda_k2,
           rel_bias, w_ff1, w_ff2, _ngt=NGT_FULL):
    x = np.asarray(x, np.float32)
    shared = _prep_shared(w_in, w_out, norm1_g, norm2_g, final_g, subln_g, lambda_q1, lambda_k1, lambda_q2,
                          lambda_k2, rel_bias, w_ff1, w_ff2)
    if _ngt not in _NC_CACHE:
        _NC_CACHE[_ngt] = build(_ngt, build(_ngt))
    nc = _NC_CACHE[_ngt]
    in_maps = []
    for c in range(N_CORES):
        m = dict(shared)
        m["x"] = np.ascontiguousarray(x[c * NSEQ_CORE:(c + 1) * NSEQ_CORE].reshape(NSEQ_CORE * SEQ, DM))
        in_maps.append(m)
    res = run_bass_kernel_spmd(nc, in_maps, core_ids=list(range(N_CORES)))
    out = np.stack([np.asarray(r["out"], np.float32).reshape(NSEQ_CORE, SEQ, DM) for r in res.results], 0)
    return out.reshape(N_CORES * NSEQ_CORE, SEQ, DM)
```

```python
import contextlib
import numpy as np
import ml_dtypes
import concourse.bass as bass
import concourse.mybir as mybir
from concourse.bass_utils import run_bass_kernel_spmd

F32 = mybir.dt.float32
BF16 = mybir.dt.bfloat16
AF = mybir.ActivationFunctionType
ALU = mybir.AluOpType
AX = mybir.AxisListType

N_CORES = 8
SEQ = 2048
DM = 1024
NSEQ_CORE = 4
GT = 512
NG_SEQ = SEQ // GT
NGT_FULL = NSEQ_CORE * NG_SEQ
NCHUNK = 24
EPS = 1e-6
NEG = -30000.0


class _Op:
    __slots__ = ("eng", "fn", "deps", "sig", "sigval", "dsem", "dval", "idx")

    def __init__(self, eng, fn, dsem=None):
        self.eng = eng
        self.fn = fn
        self.deps = []
        self.sig = False
        self.sigval = (0, 0)
        self.dsem = dsem
        self.dval = 0
        self.idx = 0


class Sched:
    ENGS = ("pe", "act", "dve", "pool", "sp")

    def __init__(self):
        self.ops = {e: [] for e in self.ENGS}
        self.last_w = {}
        self.readers = {}
        self.dsem_count = {}
        self.all_ops = []

    def op(self, eng, fn, reads=(), writes=(), dsem=None):
        o = _Op(eng, fn, dsem)
        o.idx = len(self.all_ops)
        self.all_ops.append(o)
        deps = {}
        for k in reads:
            w = self.last_w.get(k)
            if w is not None:
                deps[w.idx] = w
            if isinstance(k, tuple) and isinstance(k[0], str) and k[0].startswith("ps"):
                for sk, r in self.readers.get(k, {}).items():
                    if sk != eng:
                        deps[r.idx] = r
        for k in writes:
            w = self.last_w.get(k)
            if w is not None:
                deps[w.idx] = w
            for r in self.readers.get(k, {}).values():
                deps[r.idx] = r
        for d in deps.values():
            if d is o:
                continue
            if d.eng == eng and d.dsem is None and dsem is None:
                israw = any(self.last_w.get(k) is d for k in reads)
                if not israw or eng == "pe":
                    continue
            o.deps.append(d)
        for k in writes:
            self.last_w[k] = o
            self.readers[k] = {}
        for k in reads:
            if k in writes:
                continue
            sk = dsem if dsem is not None else eng
            self.readers.setdefault(k, {})[sk] = o
        if dsem is not None:
            self.dsem_count[dsem] = self.dsem_count.get(dsem, 0) + 16
            o.dval = self.dsem_count[dsem]
        self.ops[eng].append(o)
        return o

    EPOCH = 3000

    def finalize(self):
        for o in self.all_ops:
            for d in o.deps:
                if d.dsem is None:
                    d.sig = True
        self.nepoch = {}
        for e in self.ENGS:
            c = 0
            for o in self.ops[e]:
                if o.dsem is None and o.sig:
                    o.sigval = (c // self.EPOCH, c % self.EPOCH + 1)
                    c += 1
            self.nepoch[e] = max(1, (c + self.EPOCH - 1) // self.EPOCH)

    def emit(self, eng, handle, sems, dsems, final_waits=()):
        waited = {}
        for o in self.ops[eng]:
            need = {}
            for d in o.deps:
                if d.dsem is not None:
                    key, val = ("d", d.dsem), d.dval
                else:
                    key, val = ("e", d.eng, d.sigval[0]), d.sigval[1]
                    if any(k[0] == "e" and k[1] == d.eng and k[2] > d.sigval[0] for k in waited):
                        continue
                if waited.get(key, 0) >= val:
                    continue
                if need.get(key, 0) < val:
                    need[key] = val
            for key, val in need.items():
                s = dsems[key[1]] if key[0] == "d" else sems[key[1]][key[2]]
                handle.wait_ge(s, val)
                waited[key] = val
            ins = o.fn(handle)
            if o.dsem is not None:
                ins.then_inc(dsems[o.dsem], 16)
            elif o.sig:
                ins.then_inc(sems[eng][o.sigval[0]], 1)
        for s, v in final_waits:
            handle.wait_ge(s, v)


def build(NGT=NGT_FULL, chunk_seq_in=None):
    dry = chunk_seq_in is None
    nc = bass.Bass("TRN2", target_bir_lowering=False)
    x_d = nc.dram_tensor("x", [NSEQ_CORE * SEQ, DM], F32, kind="ExternalInput").ap()
    wall_d = nc.dram_tensor("wall", [NCHUNK, 128, 4096], F32, kind="ExternalInput").ap()
    gbc_d = nc.dram_tensor("gbc", [128, 3 * DM], F32, kind="ExternalInput").ap()
    rope_d = nc.dram_tensor("rope", [128, 16 * 32], F32, kind="ExternalInput").ap()
    rmask_d = nc.dram_tensor("rmask", [128, 6], F32, kind="ExternalInput").ap()
    ident_d = nc.dram_tensor("ident", [128, 128], BF16, kind="ExternalInput").ap()
    bias_d = nc.dram_tensor("biasg", [128, 8 * 5 * 128], F32, kind="ExternalInput").ap()
    lamv_d = nc.dram_tensor("lamv", [128, 128], F32, kind="ExternalInput").ap()
    gsub_d = nc.dram_tensor("gsub", [128, 64], F32, kind="ExternalInput").ap()
    out_d = nc.dram_tensor("out", [NSEQ_CORE * SEQ, DM], F32, kind="ExternalOutput").ap()
    wscr = nc.dram_tensor("wscr", [NCHUNK, 128, 4096], BF16).ap()

    S = Sched()
    cap_stack = []

    def OP(eng, fn, reads=(), writes=(), dsem=None):
        if dry and not cap_stack:
            return
        if cap_stack:
            cap_stack[-1].append((eng, fn, list(reads), list(writes), dsem))
        else:
            S.op(eng, fn, reads=reads, writes=writes, dsem=dsem)

    @contextlib.contextmanager
    def cap():
        lst = []
        cap_stack.append(lst)
        try:
            yield lst
        finally:
            cap_stack.pop()

    def replay(lst):
        if dry:
            return
        for (eng, fn, reads, writes, dsem) in lst:
            S.op(eng, fn, reads=reads, writes=writes, dsem=dsem)

    A = nc.alloc_sbuf_tensor
    xg = [A(f"xg{i}", [128, 4, DM], F32) for i in range(2)]
    actT = A("actT", [128, 8, GT], BF16)
    h2T = A("h2T", [128, 8, GT], BF16)
    hbf = [A(f"hbf{i}", [128, DM], BF16) for i in range(2)]
    junk = A("junk", [128, DM], BF16)
    W = [A(f"W{i}", [128, 8, 512], BF16) for i in range(3)]
    kaT = A("kaT", [128, 4, SEQ], BF16)
    va = A("va", [128, 16, 8, 65], BF16)
    kbT = A("kbT", [128, 4, 1024], BF16)
    vb = A("vb", [128, 8, 8, 65], BF16)
    qaZ = A("qaZ", [128, 4, 4, GT], BF16)
    qbZ = A("qbZ", [128, 4, 2, GT], BF16)
    rmask = A("rmask_sb", [128, 6], F32)
    ropeo = [A(f"ropeo{i}", [128, 512], BF16) for i in range(2)]
    rt = [A(f"rt{i}", [128, 256], F32) for i in range(4)]
    PT = [A(f"PT{i}", [128, 512], BF16) for i in range(3)]
    obuf = [A(f"obuf{i}", [128, 8, 64], F32) for i in range(2)]
    sqo = A("sqo", [128, 8, 64], F32)
    mix = A("mix", [128, 2, DM], BF16)
    aT = [A(f"aT{i}", [128, 8, GT], BF16) for i in range(1)]
    sq = [A(f"sq{i}", [128, 512], F32) for i in range(2)]
    gbc12 = A("gbc12_sb", [128, 2, DM], BF16)
    gbcf = A("gbcf_sb", [128, DM], F32)
    ropec = A("ropec", [128, 16, 2, 16], F32)
    ident = A("ident_sb", [128, 128], BF16)
    biasT = A("biasT", [128, 8, 640], BF16)
    lamv = A("lamv_sb", [128, 4, 32], F32)
    lamt = A("lamt", [128, 2, 32], F32)
    lams = A("lams", [128, 8], F32)
    gsub = A("gsub_sb", [128, 64], F32)
    st = A("st", [128, 3, 4, 4], F32)
    rec = [A(f"rec{i}", [128, 2], F32) for i in range(2)]
    t1 = [A(f"t1_{i}", [128, 64], F32) for i in range(2)]
    recb = [A(f"recb{i}", [128, 1], F32) for i in range(2)]
    so = A("so", [128, 3, 8], F32)
    maskc = A("maskc", [128, 1], F32)

    PS = [nc.alloc_psum_tensor(f"psf{i}", [128, 512], F32) for i in (0, 1)]
    PS_TP = [nc.alloc_psum_tensor(f"pstp{i}", [128, 1024], BF16) for i in (0, 1)]
    PS_S = [nc.alloc_psum_tensor(f"pss{i}", [128, 512], F32) for i in (0, 1)]
    PS_ACC = [nc.alloc_psum_tensor(f"psa{i}", [128, 512], F32) for i in (0, 1)]

    cnt = {}

    def rot(name, n):
        v = cnt.get(name, 0) % n
        cnt[name] = cnt.get(name, 0) + 1
        return v

    def I(name, *args, **kwargs):
        return lambda e: getattr(e, name)(*args, **kwargs)

    OP("sp", I("dma_start", out=ident[:], in_=ident_d), writes=["ident"], dsem="dc0")
    OP("pool", I("dma_start", out=gbc12[:], in_=gbc_d[:, 0:2 * DM].rearrange("p (a d) -> p a d", a=2)),
       writes=["gbc12"], dsem="dc1")
    OP("sp", I("dma_start", out=gbcf[:], in_=gbc_d[:, 2 * DM:3 * DM]), writes=["gbcf"], dsem="dc6")
    OP("sp", I("dma_start", out=rmask[:], in_=rmask_d), writes=["rmask"], dsem="dc7")
    OP("sp", I("dma_start", out=ropec[:].rearrange("p a b c -> p (a b c)"), in_=rope_d), writes=["ropec"],
         dsem="dc2")
    OP("sp", I("dma_start", out=lamv[:].rearrange("p a d -> p (a d)"), in_=lamv_d), writes=["lamv"], dsem="dc3")
    OP("sp", I("dma_start", out=gsub[:], in_=gsub_d), writes=["gsub"], dsem="dc4")
    OP("pool", I("dma_start", out=biasT[:], in_=bias_d.rearrange("p (h q) -> p h q", h=8)), writes=["biasT"],
         dsem="dc5")
    OP("pool", I("memset", maskc[0:64, :], 0.0), writes=["maskc_lo"])
    OP("pool", I("memset", maskc[64:128, :], NEG), writes=["maskc"])
    OP("dve", I("tensor_scalar", out=biasT[:], in0=biasT[:], scalar1=8.0, scalar2=None, op0=ALU.mult),
       reads=["biasT"], writes=["bias8"])
    OP("pool", I("memset", va[:].rearrange("p a h e -> p (a h) e")[:, :, 64:65], 1.0), writes=["va_ones"])
    OP("pool", I("memset", vb[:].rearrange("p a h e -> p (a h) e")[:, :, 64:65], 1.0), writes=["vb_ones"])
    OP("dve", I("tensor_tensor", out=lamt[:, 0, :], in0=lamv[:, 0, :], in1=lamv[:, 1, :], op=ALU.mult),
         reads=["lamv"], writes=["lamt0"])
    OP("dve", I("tensor_tensor", out=lamt[:, 1, :], in0=lamv[:, 2, :], in1=lamv[:, 3, :], op=ALU.mult),
         reads=["lamv"], writes=["lamt1"])
    OP("dve", I("tensor_reduce", out=lams[:, 0:2], in_=lamt[:], axis=AX.X, op=ALU.add),
         reads=["lamt0", "lamt1"], writes=["lams01"])
    OP("act", I("activation", out=lams[:, 2:4], in_=lams[:, 0:2], func=AF.Exp), reads=["lams01"],
         writes=["lams23"])
    OP("dve", I("scalar_tensor_tensor", out=lams[:, 4:5], in0=lams[:, 3:4], scalar=-0.2, in1=lams[:, 2:3],
                  op0=ALU.add, op1=ALU.subtract), reads=["lams23"], writes=["neglam"])
    OP("dve", I("tensor_scalar", out=gsub[:], in0=gsub[:], scalar1=0.8, scalar2=None, op0=ALU.mult),
         reads=["gsub"], writes=["gsub"])
    neglam = lams[:, 4:5]

    for c in range(NCHUNK):
        OP("pool", I("dma_start", out=wscr[c].rearrange("p (a n) -> p a n", a=8),
                       in_=wall_d[c].rearrange("p (a n) -> p a n", a=8)),
             writes=[("wscr", c)], dsem=f"dw{c}")

    wstate = {"next": 0, "cons": 0}
    chunk_seq = [] if dry else list(chunk_seq_in)

    def issue_wload(n):
        c = chunk_seq[n]
        slot = n % 3
        OP("sp", I("dma_start", out=W[slot][:].rearrange("p a n -> p (a n)"), in_=wscr[c]),
           reads=[("wscr", c)], writes=[("W", slot)], dsem=f"dW{slot}")

    def get_chunk(c):
        n = wstate["cons"]
        wstate["cons"] += 1
        if dry:
            chunk_seq.append(c)
            return n % 3
        assert chunk_seq[n] == c, (n, chunk_seq[n], c)
        assert not cap_stack
        while wstate["next"] <= min(n + 2, len(chunk_seq) - 1):
            issue_wload(wstate["next"])
            wstate["next"] += 1
        return n % 3

    def issue_xload(G):
        slot = G % 2
        src = x_d[G * GT:(G + 1) * GT, :].rearrange("(t p) d -> p t d", p=128)
        OP("sp", I("dma_start", out=xg[slot][:], in_=src),
             writes=[("xg", slot, t) for t in range(4)], dsem=f"dx{slot}")

    def transpose_masked(src, src_keys, dstZ, ngrp, mcol0, t, dkey):
        b = rot("tp", 2)
        for i in range(4):
            OP("pe", I("transpose", out=PS_TP[b][:, i * 128:(i + 1) * 128], in_=src[:, i * 128:(i + 1) * 128],
                       identity=ident[:]), reads=list(src_keys) + ["ident"], writes=[("pstp", b)])
        src_ps = PS_TP[b][:, 0:512].rearrange("p (a b) -> p a b", a=4)
        use_dve = rot("tm", 2) == 0
        for r in range(ngrp):
            dst = dstZ[:, :, r, t * 128:(t + 1) * 128]
            if use_dve:
                OP("dve", I("tensor_scalar", out=dst, in0=src_ps, scalar1=rmask[:, mcol0 + r:mcol0 + r + 1],
                            scalar2=None, op0=ALU.mult), reads=[("pstp", b), "rmask"], writes=[(dkey, t, r)])
            else:
                OP("act", I("mul", out=dst, in_=src_ps, mul=rmask[:, mcol0 + r:mcol0 + r + 1]),
                   reads=[("pstp", b), "rmask"], writes=[(dkey, t, r)])

    def transpose_to(src, src_keys, nblk, dst_ap, dst_keys, eng):
        b = rot("tp", 2)
        for i in range(nblk):
            OP("pe", I("transpose", out=PS_TP[b][:, i * 128:(i + 1) * 128], in_=src[:, i * 128:(i + 1) * 128],
                         identity=ident[:]), reads=list(src_keys) + ["ident"], writes=[("pstp", b)])
        src_ps = PS_TP[b][:, 0:nblk * 128].rearrange("p (a b) -> p a b", a=nblk)
        if eng == "act":
            OP("act", I("activation", out=dst_ap, in_=src_ps, func=AF.Copy), reads=[("pstp", b)], writes=dst_keys)
        else:
            OP("dve", I("tensor_copy", out=dst_ap, in_=src_ps), reads=[("pstp", b)], writes=dst_keys)

    def rms_stats(xs_t, xkey, kind, t, hs=None):
        OP("act", I("activation", out=junk[:], in_=xs_t, func=AF.Square, accum_out=st[:, kind, t, 0:1]),
             reads=[xkey], writes=["junk", ("st", kind, t, 0)])
        OP("act", I("activation", out=st[:, kind, t, 1:2], in_=st[:, kind, t, 0:1], func=AF.Ln,
                      scale=1.0 / DM, bias=EPS), reads=[("st", kind, t, 0)], writes=[("st", kind, t, 1)])
        OP("act", I("activation", out=st[:, kind, t, 2:3], in_=st[:, kind, t, 1:2], func=AF.Exp, scale=-0.5),
             reads=[("st", kind, t, 1)], writes=[("st", kind, t, 2)])

    def norm_to_actT(G, kind, t, dstT=None, dkey="actT"):
        dstT = actT if dstT is None else dstT
        xs_t = xg[G % 2][:, t, :]
        xkey = ("xg", G % 2, t)
        hs = rot("hbf", 2)
        rms_stats(xs_t, xkey, kind, t, hs)
        OP("dve", I("scalar_tensor_tensor", out=hbf[hs][:], in0=xs_t, scalar=st[:, kind, t, 2:3],
                      in1=gbc12[:, kind, :], op0=ALU.mult, op1=ALU.mult),
             reads=[xkey, ("st", kind, t, 2), "gbc12"], writes=[("hbf", hs)])
        transpose_to(hbf[hs], [("hbf", hs)], 8, dstT[:, :, t * 128:(t + 1) * 128], [(dkey, t)], "act")

    def norm_phase(G, kind, dstT, dkey, filler=None, nfill=0):
        xs = xg[G % 2]
        for t in range(4):
            rms_stats(xs[:, t, :], ("xg", G % 2, t), kind, t)
        pend = None
        for t in range(4):
            hs = rot("hbf", 2)
            OP("dve", I("scalar_tensor_tensor", out=hbf[hs][:], in0=xs[:, t, :], scalar=st[:, kind, t, 2:3],
                        in1=gbc12[:, kind, :], op0=ALU.mult, op1=ALU.mult),
               reads=[("xg", G % 2, t), ("st", kind, t, 2), "gbc12"], writes=[("hbf", hs)])
            if pend is not None:
                replay(pend)
            b = rot("tp", 2)
            for i in range(8):
                OP("pe", I("transpose", out=PS_TP[b][:, i * 128:(i + 1) * 128], in_=hbf[hs][:, i * 128:(i + 1) * 128],
                           identity=ident[:]), reads=[("hbf", hs), "ident"], writes=[("pstp", b)])
            with cap() as pend:
                OP("dve", I("tensor_copy", out=dstT[:, :, t * 128:(t + 1) * 128],
                            in_=PS_TP[b][:, :].rearrange("p (a b) -> p a b", a=8)),
                   reads=[("pstp", b)], writes=[(dkey, t)])
            if filler is not None:
                for _ in range(nfill):
                    next(filler, None)
        replay(pend)

    def dense_tokmajor(wslot, lhs_list, lhs_keys):
        b = rot("mm", 2)
        nk = len(lhs_list)
        for k in range(nk):
            OP("pe", I("matmul", PS[b][:, :], lhsT=lhs_list[k], rhs=W[wslot][:, k, :], start=(k == 0),
                         stop=(k == nk - 1)), reads=list(lhs_keys) + [("W", wslot)], writes=[("psf", b)])
        return b

    def residual_add(G, b, t, dh):
        xs = xg[G % 2]
        OP("dve", I("tensor_tensor", out=xs[:, t, dh * 512:(dh + 1) * 512], in0=PS[b][:, :],
                      in1=xs[:, t, dh * 512:(dh + 1) * 512], op=ALU.add),
             reads=[("psf", b), ("xg", G % 2, t)], writes=[("xg", G % 2, t)])

    def inproj(G, cg, ws, t):
        gs = G % NG_SEQ
        jt = gs * 4 + t
        b = dense_tokmajor(ws, [actT[:, k, t * 128:(t + 1) * 128] for k in range(8)], [("actT", t)])
        ps = PS[b]
        if cg in (0, 1):
            v = ps[:, :].rearrange("p (s h j) -> p s h j", s=16, h=2, j=16)
            x1, x2 = v[:, :, 0, :], v[:, :, 1, :]
            cosb = ropec[:, jt, 0, :].unsqueeze(1).to_broadcast([128, 16, 16])
            sinb = ropec[:, jt, 1, :].unsqueeze(1).to_broadcast([128, 16, 16])
            r3 = [r[:].rearrange("p (s j) -> p s j", s=16) for r in rt]
            for i, (a_, b_) in enumerate([(x1, cosb), (x2, sinb), (x2, cosb), (x1, sinb)]):
                OP("dve", I("tensor_tensor", out=r3[i], in0=a_, in1=b_, op=ALU.mult),
                     reads=[("psf", b), "ropec"], writes=[f"rt{i}"])
            ro = rot("ropeo", 2)
            rov = ropeo[ro][:].rearrange("p (s h j) -> p s h j", s=16, h=2, j=16)
            OP("pool", I("tensor_tensor", out=rov[:, :, 0, :], in0=r3[0], in1=r3[1], op=ALU.subtract),
                 reads=["rt0", "rt1"], writes=[("ropeo", ro, 0)])
            OP("pool", I("tensor_tensor", out=rov[:, :, 1, :], in0=r3[2], in1=r3[3], op=ALU.add),
                 reads=["rt2", "rt3"], writes=[("ropeo", ro, 1)])
            with cap() as deferred:
                if cg == 0:
                    transpose_masked(ropeo[ro], [("ropeo", ro, 0), ("ropeo", ro, 1)], qaZ, 4, 0, t, "qaZ")
                else:
                    dst, dk = kaT[:, :, jt * 128:(jt + 1) * 128], [("kaT", jt)]
                    transpose_to(ropeo[ro], [("ropeo", ro, 0), ("ropeo", ro, 1)], 4, dst, dk, "dve")
            return deferred
        elif cg in (3, 4):
            ro = rot("ropeo", 2)
            OP("act", I("activation", out=ropeo[ro][:], in_=ps[:, :], func=AF.Copy),
                 reads=[("psf", b)], writes=[("ropeo", ro, 0), ("ropeo", ro, 1)])
            with cap() as deferred:
                if cg == 3:
                    transpose_masked(ropeo[ro], [("ropeo", ro, 0), ("ropeo", ro, 1)], qbZ, 2, 4, t, "qbZ")
                else:
                    sl = jt % 8
                    dst, dk = kbT[:, :, sl * 128:(sl + 1) * 128], [("kbT", sl)]
                    transpose_to(ropeo[ro], [("ropeo", ro, 0), ("ropeo", ro, 1)], 4, dst, dk, "dve")
            return deferred
        else:
            pv = ps[:, :].rearrange("p (h d) -> p h d", h=8)
            if cg == 2:
                dst, dk = va[:, jt, :, 0:64], [("va", jt)]
            else:
                sl = jt % 8
                dst, dk = vb[:, sl, :, 0:64], [("vb", sl)]
            OP("act", I("activation", out=dst, in_=pv, func=AF.Copy), reads=[("psf", b)], writes=dk)
            return []

    def attn_units(G):
        units = []
        for t in range(4):
            qb = (G % NG_SEQ) * 4 + t
            ob = rot("ob", 2)
            jlo = max(0, qb - 4)

            def a_head(h, t=t, qb=qb, ob=ob, jlo=jlo):
                ab = rot("acc", 2)
                njt = qb + 1
                for c in range(2):
                    s = 2 * h + c
                    rg, ti = s % 4, s // 4
                    for j0 in range(0, njt, 4):
                        n = min(4, njt - j0)
                        sb = rot("s", 2)
                        pt = rot("pt", 3)
                        u = {}
                        with cap() as u["S"]:
                            for jj in range(n):
                                j = j0 + jj
                                OP("pe", I("matmul", PS_S[sb][:, jj * 128:(jj + 1) * 128],
                                           lhsT=kaT[:, ti, j * 128:(j + 1) * 128],
                                           rhs=qaZ[:, ti, rg, t * 128:(t + 1) * 128], start=True, stop=True),
                                   reads=[("kaT", j), ("qaZ", t, rg)], writes=[("pss", sb)])
                        with cap() as u["post"]:
                            OP("act", I("activation", out=PT[pt][:, 0:n * 128], in_=PS_S[sb][:, 0:n * 128],
                                        func=AF.Exp, scale=32 ** -0.5), reads=[("pss", sb)], writes=[("PT", pt)])
                            if j0 + n == njt:
                                c0 = (n - 1) * 128
                                OP("act", I("activation", out=PT[pt][:, c0:c0 + 64], in_=PS_S[sb][:, c0:c0 + 64],
                                            func=AF.Exp, scale=32 ** -0.5, bias=maskc[:, 0:1]),
                                   reads=[("pss", sb), "maskc"], writes=[("PT", pt)])
                        with cap() as u["PV"]:
                            for jj in range(n):
                                j = j0 + jj
                                OP("pe", I("matmul", PS_ACC[ab][:, c * 65:(c + 1) * 65],
                                           lhsT=PT[pt][:, jj * 128:(jj + 1) * 128], rhs=va[:, j, h, :],
                                           start=(j == 0), stop=(j == njt - 1)),
                                   reads=[("PT", pt), ("va", j), "va_ones"], writes=[("psa", ab)])
                        with cap() as u["fin"]:
                            if c == 1 and j0 + n == njt:
                                rc = rot("rec", 2)
                                accv = PS_ACC[ab][:, 0:130].rearrange("p (c e) -> p c e", c=2)
                                OP("dve", I("reciprocal", out=rec[rc][:], in_=accv[:, :, 64]), reads=[("psa", ab)],
                                   writes=[("rec", rc)])
                                OP("dve", I("tensor_scalar", out=t1[rc][:], in0=PS_ACC[ab][:, 65:129],
                                            scalar1=rec[rc][:, 1:2], scalar2=neglam, op0=ALU.mult, op1=ALU.mult),
                                   reads=[("psa", ab), ("rec", rc), "neglam"], writes=[("t1", rc)])
                                OP("dve", I("scalar_tensor_tensor", out=obuf[ob][:, h, :], in0=PS_ACC[ab][:, 0:64],
                                            scalar=rec[rc][:, 0:1], in1=t1[rc][:], op0=ALU.mult, op1=ALU.add),
                                   reads=[("psa", ab), ("rec", rc), ("t1", rc)], writes=[("obuf", ob)])
                                if h == 7:
                                    OP("pool", I("tensor_tensor", out=sqo[:], in0=obuf[ob][:], in1=obuf[ob][:],
                                                 op=ALU.mult), reads=[("obuf", ob)], writes=["sqo"])
                                    OP("dve", I("tensor_reduce", out=so[:, 0, :], in_=sqo[:], axis=AX.X, op=ALU.add),
                                       reads=["sqo"], writes=["so0"])
                        if h == 7 and c == 1 and j0 + n == njt:
                            with cap() as u["late1"]:
                                OP("act", I("activation", out=so[:, 1, :], in_=so[:, 0, :], func=AF.Ln,
                                            scale=1.0 / 64, bias=EPS), reads=["so0"], writes=["so1"])
                                OP("act", I("activation", out=so[:, 2, :], in_=so[:, 1, :], func=AF.Exp,
                                            scale=-0.5), reads=["so1"], writes=["so2"])
                                OP("pool", I("tensor_tensor", out=sqo[:], in0=obuf[ob][:],
                                             in1=so[:, 2, :].unsqueeze(2).to_broadcast([128, 8, 64]), op=ALU.mult),
                                   reads=[("obuf", ob), "so2"], writes=["sqo"])
                                OP("pool", I("tensor_tensor",
                                             out=mix[:, t % 2, 0:512].rearrange("p (h d) -> p h d", h=8), in0=sqo[:],
                                             in1=gsub[:].unsqueeze(1).to_broadcast([128, 8, 64]), op=ALU.mult),
                                   reads=["sqo", "gsub"], writes=[("mixA", t % 2)])
                        units.append(u)

            def b_head(h, t=t, qb=qb, ob=ob, jlo=jlo):
                ab = rot("acc", 2)
                rg, ti = h % 2, h // 2
                njt = qb - jlo + 1
                for j0 in range(0, njt, 4):
                    n = min(4, njt - j0)
                    sb = rot("s", 2)
                    pt = rot("pt", 3)
                    u = {}
                    with cap() as u["S"]:
                        jb = (jlo + j0) - qb + 4
                        OP("pe", I("matmul", PS_S[sb][:, 0:n * 128], lhsT=ident[:],
                                   rhs=biasT[:, h, jb * 128:(jb + n) * 128], start=True, stop=False),
                           reads=["ident", "bias8"], writes=[("pss", sb)])
                        for jj in range(n):
                            sl = (jlo + j0 + jj) % 8
                            OP("pe", I("matmul", PS_S[sb][:, jj * 128:(jj + 1) * 128],
                                       lhsT=kbT[:, ti, sl * 128:(sl + 1) * 128],
                                       rhs=qbZ[:, ti, rg, t * 128:(t + 1) * 128], start=False, stop=(jj == n - 1)),
                               reads=[("kbT", sl), ("qbZ", t, rg)], writes=[("pss", sb)])
                    with cap() as u["post"]:
                        OP("act", I("activation", out=PT[pt][:, 0:n * 128], in_=PS_S[sb][:, 0:n * 128], func=AF.Exp,
                                    scale=0.125), reads=[("pss", sb)], writes=[("PT", pt)])
                    with cap() as u["PV"]:
                        for jj in range(n):
                            j = jlo + j0 + jj
                            sl = j % 8
                            OP("pe", I("matmul", PS_ACC[ab][:, 0:65], lhsT=PT[pt][:, jj * 128:(jj + 1) * 128],
                                       rhs=vb[:, sl, h, :], start=(j == jlo), stop=(j == qb)),
                               reads=[("PT", pt), ("vb", sl), "vb_ones"], writes=[("psa", ab)])
                    with cap() as u["fin"]:
                        if j0 + n == njt:
                            rb = rot("recb", 2)
                            OP("dve", I("reciprocal", out=recb[rb][:], in_=PS_ACC[ab][:, 64:65]),
                               reads=[("psa", ab)], writes=[("recb", rb)])
                            OP("dve", I("tensor_scalar", out=mix[:, t % 2, 512 + h * 64:512 + (h + 1) * 64],
                                        in0=PS_ACC[ab][:, 0:64], scalar1=recb[rb][:, 0:1], scalar2=None,
                                        op0=ALU.mult), reads=[("psa", ab), ("recb", rb)], writes=[("mixB", t % 2)])
                    if h == 7 and j0 + n == njt:
                        with cap() as u["late"]:
                            transpose_to(mix[:, t % 2, :], [("mixA", t % 2), ("mixB", t % 2)], 8,
                                         actT[:, :, t * 128:(t + 1) * 128], [("actT", t)], "act")
                    units.append(u)

            for h in range(8):
                a_head(h)
                b_head(h)
        return units

    def ffn_gen(Gp):
        order = [(1, 0, 8), (2, 0, 10), (1, 1, 12), (2, 1, 14), (1, 2, 16), (2, 2, 18), (1, 3, 20), (2, 3, 22)]
        for kind, r, cidx in order:
            if kind == 1:
                for half in range(2):
                    ws = get_chunk(cidx + half)
                    for f4 in range(4):
                        f8 = half * 4 + f4
                        b = rot("mm", 2)
                        for k in range(8):
                            OP("pe", I("matmul", PS[b][:, :], lhsT=W[ws][:, k, f4 * 128:(f4 + 1) * 128],
                                       rhs=h2T[:, k, :], start=(k == 0), stop=(k == 7)),
                               reads=[("W", ws)] + [("h2T", tt) for tt in range(4)], writes=[("psf", b)])
                            if k == 7:
                                sqs = rot("sq", 2)
                                OP("dve", I("tensor_scalar", out=sq[sqs][:], in0=PS[b][:, :], scalar1=0.0,
                                            scalar2=None, op0=ALU.max), reads=[("psf", b)], writes=[("sq", sqs)])
                                OP("pool", I("tensor_tensor", out=aT[0][:, f8, :], in0=sq[sqs][:],
                                             in1=sq[sqs][:], op=ALU.mult), reads=[("sq", sqs)],
                                   writes=[("aT", 0, f8)])
                            yield
            else:
                for dh in range(2):
                    ws = get_chunk(cidx + dh)
                    for t in range(4):
                        b = rot("mm", 2)
                        for k in range(8):
                            OP("pe", I("matmul", PS[b][:, :], lhsT=aT[0][:, k, t * 128:(t + 1) * 128],
                                       rhs=W[ws][:, k, :], start=(k == 0), stop=(k == 7)),
                               reads=[("aT", 0, k), ("W", ws)], writes=[("psf", b)])
                            if k == 7:
                                residual_add(Gp, b, t, dh)
                            yield
        final_norm_store(Gp)
        if Gp + 2 < NGT:
            issue_xload(Gp + 2)
        yield

    def final_norm_store(G):
        xs = xg[G % 2]
        for t in range(4):
            rms_stats(xs[:, t, :], ("xg", G % 2, t), 2, t)
        for t in range(4):
            OP("dve", I("scalar_tensor_tensor", out=xs[:, t, :], in0=xs[:, t, :], scalar=st[:, 2, t, 2:3],
                        in1=gbcf[:], op0=ALU.mult, op1=ALU.mult),
               reads=[("xg", G % 2, t), ("st", 2, t, 2), "gbcf"], writes=[("xg", G % 2, t)])
        dst = out_d[G * GT:(G + 1) * GT, :].rearrange("(t p) d -> p t d", p=128)
        OP("pool", I("dma_start", out=dst, in_=xs[:]), reads=[("xg", G % 2, t) for t in range(4)],
           dsem=f"do{G % 2}")

    issue_xload(0)
    if NGT > 1:
        issue_xload(1)
    N_DENSE = 513
    N_FILL = 8
    norm_phase(0, 0, actT, "actT")
    dense = iter(())
    n_left = 0
    for G in range(NGT):
        pending = []
        for cg in range(6):
            ws = get_chunk(cg)
            for t in range(4):
                nxt = inproj(G, cg, ws, t)
                replay(pending)
                pending = nxt
        replay(pending)
        units = attn_units(G)
        n = len(units)
        replay(units[0]["S"])
        accf = 0.0
        rate = n_left * 1.0 / n
        LATE = 10
        LATE1 = 5
        for ui in range(n):
            if ui + 1 < n:
                replay(units[ui + 1]["S"])
            replay(units[ui]["post"])
            accf += rate
            while accf >= 1.0:
                accf -= 1.0
                next(dense, None)
            replay(units[ui]["PV"])
            replay(units[ui]["fin"])
            if ui >= LATE1 and "late1" in units[ui - LATE1]:
                replay(units[ui - LATE1]["late1"])
            if ui >= LATE and "late" in units[ui - LATE]:
                replay(units[ui - LATE]["late"])
        for ui in range(max(0, n - LATE1), n):
            if "late1" in units[ui]:
                replay(units[ui]["late1"])
        for ui in range(max(0, n - LATE), n):
            if "late" in units[ui]:
                replay(units[ui]["late"])
        for _ in dense:
            pass
        for dh in range(2):
            ws = get_chunk(6 + dh)
            for t in range(4):
                b = dense_tokmajor(ws, [actT[:, k, t * 128:(t + 1) * 128] for k in range(8)], [("actT", t)])
                residual_add(G, b, t, dh)
        norm_phase(G, 1, h2T, "h2T")
        dense = ffn_gen(G)
        n_left = N_DENSE
        if G + 1 < NGT:
            norm_phase(G + 1, 0, actT, "actT", dense, N_FILL)
            n_left -= 4 * N_FILL
    for _ in dense:
        pass
    if dry:
        return chunk_seq

    S.finalize()
    dsem_names = sorted(S.dsem_count.keys())
    with contextlib.ExitStack() as es:
        sems = {e: [es.enter_context(nc.semaphore(f"s_{e}{i}")) for i in range(S.nepoch[e])]
                for e in Sched.ENGS}
        dsems = {k: es.enter_context(nc.semaphore(f"ds_{k}")) for k in dsem_names}
        block = es.enter_context(nc.Block())
        finals = [(dsems[k], S.dsem_count[k]) for k in ("do0", "do1") if k in S.dsem_count]

        @block.sync
        def _(eng):
            S.emit("sp", eng, sems, dsems)

        @block.scalar
        def _(eng):
            S.emit("act", eng, sems, dsems)

        @block.vector
        def _(eng):
            S.emit("dve", eng, sems, dsems)

        @block.gpsimd
        def _(eng):
            S.emit("pool", eng, sems, dsems, final_waits=finals)

        @block.tensor
        def _(eng):
            S.emit("pe", eng, sems, dsems)
    return nc


def _chunk_k(w, c0):
    return np.ascontiguousarray(w[:, c0:c0 + 512].reshape(8, 128, 512).transpose(1, 0, 2)).reshape(128, 4096)


def _prep_shared(w_in, w_out, norm1_g, norm2_g, final_g, subln_g, lambda_q1, lambda_k1, lambda_q2, lambda_k2,
                 rel_bias, w_ff1, w_ff2):
    w_in, w_out, w_ff1, w_ff2 = (np.asarray(a, np.float32)[0] for a in (w_in, w_out, w_ff1, w_ff2))
    chunks = [_chunk_k(w_in, c * 512) for c in range(6)] + [_chunk_k(w_out, c * 512) for c in range(2)]

    def f1(r):
        return [_chunk_k(w_ff1, r * 1024 + hh * 512) for hh in range(2)]

    def f2(r):
        return [_chunk_k(w_ff2[r * 1024:(r + 1) * 1024], dh * 512) for dh in range(2)]

    chunks += f1(0) + f2(0) + f1(1) + f2(1) + f1(2) + f2(2) + f1(3) + f2(3)
    wall = np.stack(chunks, 0)
    gbc = np.concatenate([np.broadcast_to(np.asarray(g, np.float32).reshape(1, DM), (128, DM))
                          for g in (norm1_g, norm2_g, final_g)], axis=1)
    half = 16
    inv_freq = (np.float32(10000.0) ** (-np.arange(half, dtype=np.float32) / np.float32(half))).astype(np.float32)
    pos = np.arange(SEQ, dtype=np.float32)
    ang = pos[:, None] * inv_freq[None, :]
    cs = np.stack([np.cos(ang), np.sin(ang)], axis=1).astype(np.float32)
    rope = np.ascontiguousarray(cs.reshape(16, 128, 2, 16).transpose(1, 0, 2, 3)).reshape(128, 512)
    ident = np.eye(128, dtype=np.float32).astype(ml_dtypes.bfloat16)
    tab = np.concatenate([np.asarray(rel_bias, np.float32)[0], np.full((8, 1), NEG, np.float32)], axis=1)
    k = np.arange(128)[:, None, None]
    jb = np.arange(5)[None, :, None]
    q = np.arange(128)[None, None, :]
    r = 4 - jb
    idx = np.clip(r * 128 + q - k, -256, 256) + 256
    masked = ((r == 0) & (q < 64) & (k >= 64)) | ((r == 4) & (q >= 64) & (k < 64))
    idx = np.where(masked, 513, idx)
    biasg = np.ascontiguousarray(tab[:, idx].transpose(1, 0, 2, 3)).reshape(128, 8 * 5 * 128)
    lamv = np.concatenate([np.broadcast_to(np.asarray(v, np.float32).reshape(1, 32), (128, 32))
                           for v in (lambda_q1, lambda_k1, lambda_q2, lambda_k2)], axis=1)
    gsub = np.ascontiguousarray(np.broadcast_to(np.asarray(subln_g, np.float32).reshape(1, 64), (128, 64)))
    rmask = np.zeros((128, 6), np.float32)
    for r in range(4):
        rmask[32 * r:32 * r + 32, r] = 1.0
    for r in range(2):
        rmask[64 * r:64 * r + 64, 4 + r] = 1.0
    return {"rmask": rmask, "wall": wall, "gbc": np.ascontiguousarray(gbc), "rope": rope, "ident": ident, "biasg": biasg,
            "lamv": np.ascontiguousarray(lamv), "gsub": gsub}


_NC_CACHE = {}


def kernel(x, w_in, w_out, norm1_g, norm2_g, final_g, subln_g, lambda_q1, lambda_k1, lambda_q2, lambda_k2,
           rel_bias, w_ff1, w_ff2, _ngt=NGT_FULL):
    x = np.asarray(x, np.float32)
    shared = _prep_shared(w_in, w_out, norm1_g, norm2_g, final_g, subln_g, lambda_q1, lambda_k1, lambda_q2,
                          lambda_k2, rel_bias, w_ff1, w_ff2)
    if _ngt not in _NC_CACHE:
        _NC_CACHE[_ngt] = build(_ngt, build(_ngt))
    nc = _NC_CACHE[_ngt]
    in_maps = []
    for c in range(N_CORES):
        m = dict(shared)
        m["x"] = np.ascontiguousarray(x[c * NSEQ_CORE:(c + 1) * NSEQ_CORE].reshape(NSEQ_CORE * SEQ, DM))
        in_maps.append(m)
    res = run_bass_kernel_spmd(nc, in_maps, core_ids=list(range(N_CORES)))
    out = np.stack([np.asarray(r["out"], np.float32).reshape(NSEQ_CORE, SEQ, DM) for r in res.results], 0)
    return out.reshape(N_CORES * NSEQ_CORE, SEQ, DM)
```

```python
import contextlib
import numpy as np
import ml_dtypes
import concourse.bass as bass
import concourse.mybir as mybir
from concourse.bass_utils import run_bass_kernel_spmd

F32 = mybir.dt.float32
BF16 = mybir.dt.bfloat16
AF = mybir.ActivationFunctionType
ALU = mybir.AluOpType
AX = mybir.AxisListType

N_CORES = 8
SEQ = 2048
DM = 1024
NSEQ_CORE = 4
GT = 512
NG_SEQ = SEQ // GT
NGT_FULL = NSEQ_CORE * NG_SEQ
NCHUNK = 24
EPS = 1e-6
NEG = -30000.0


class _Op:
    __slots__ = ("eng", "fn", "deps", "sig", "sigval", "dsem", "dval", "idx")

    def __init__(self, eng, fn, dsem=None):
        self.eng = eng
        self.fn = fn
        self.deps = []
        self.sig = False
        self.sigval = (0, 0)
        self.dsem = dsem
        self.dval = 0
        self.idx = 0


class Sched:
    ENGS = ("pe", "act", "dve", "pool", "sp")

    def __init__(self):
        self.ops = {e: [] for e in self.ENGS}
        self.last_w = {}
        self.readers = {}
        self.dsem_count = {}
        self.all_ops = []

    def op(self, eng, fn, reads=(), writes=(), dsem=None):
        o = _Op(eng, fn, dsem)
        o.idx = len(self.all_ops)
        self.all_ops.append(o)
        deps = {}
        for k in reads:
            w = self.last_w.get(k)
            if w is not None:
                deps[w.idx] = w
            if isinstance(k, tuple) and isinstance(k[0], str) and k[0].startswith("ps"):
                for sk, r in self.readers.get(k, {}).items():
                    if sk != eng:
                        deps[r.idx] = r
        for k in writes:
            w = self.last_w.get(k)
            if w is not None:
                deps[w.idx] = w
            for r in self.readers.get(k, {}).values():
                deps[r.idx] = r
        for d in deps.values():
            if d is o:
                continue
            if d.eng == eng and d.dsem is None and dsem is None:
                israw = any(self.last_w.get(k) is d for k in reads)
                if not israw or eng == "pe":
                    continue
            o.deps.append(d)
        for k in writes:
            self.last_w[k] = o
            self.readers[k] = {}
        for k in reads:
            if k in writes:
                continue
            sk = dsem if dsem is not None else eng
            self.readers.setdefault(k, {})[sk] = o
        if dsem is not None:
            self.dsem_count[dsem] = self.dsem_count.get(dsem, 0) + 16
            o.dval = self.dsem_count[dsem]
        self.ops[eng].append(o)
        return o

    EPOCH = 3000

    def finalize(self):
        for o in self.all_ops:
            for d in o.deps:
                if d.dsem is None:
                    d.sig = True
        self.nepoch = {}
        for e in self.ENGS:
            c = 0
            for o in self.ops[e]:
                if o.dsem is None and o.sig:
                    o.sigval = (c // self.EPOCH, c % self.EPOCH + 1)
                    c += 1
            self.nepoch[e] = max(1, (c + self.EPOCH - 1) // self.EPOCH)

    def emit(self, eng, handle, sems, dsems, final_waits=()):
        waited = {}
        for o in self.ops[eng]:
            need = {}
            for d in o.deps:
                if d.dsem is not None:
                    key, val = ("d", d.dsem), d.dval
                else:
                    key, val = ("e", d.eng, d.sigval[0]), d.sigval[1]
                    if any(k[0] == "e" and k[1] == d.eng and k[2] > d.sigval[0] for k in waited):
                        continue
                if waited.get(key, 0) >= val:
                    continue
                if need.get(key, 0) < val:
                    need[key] = val
            for key, val in need.items():
                s = dsems[key[1]] if key[0] == "d" else sems[key[1]][key[2]]
                handle.wait_ge(s, val)
                waited[key] = val
            ins = o.fn(handle)
            if o.dsem is not None:
                ins.then_inc(dsems[o.dsem], 16)
            elif o.sig:
                ins.then_inc(sems[eng][o.sigval[0]], 1)
        for s, v in final_waits:
            handle.wait_ge(s, v)


def build(NGT=NGT_FULL, chunk_seq_in=None):
    dry = chunk_seq_in is None
    nc = bass.Bass("TRN2", target_bir_lowering=False)
    x_d = nc.dram_tensor("x", [NSEQ_CORE * SEQ, DM], F32, kind="ExternalInput").ap()
    wall_d = nc.dram_tensor("wall", [NCHUNK, 128, 4096], F32, kind="ExternalInput").ap()
    gbc_d = nc.dram_tensor("gbc", [128, 3 * DM], F32, kind="ExternalInput").ap()
    rope_d = nc.dram_tensor("rope", [128, 16 * 32], F32, kind="ExternalInput").ap()
    rmask_d = nc.dram_tensor("rmask", [128, 6], F32, kind="ExternalInput").ap()
    ident_d = nc.dram_tensor("ident", [128, 128], BF16, kind="ExternalInput").ap()
    bias_d = nc.dram_tensor("biasg", [128, 8 * 5 * 128], F32, kind="ExternalInput").ap()
    lamv_d = nc.dram_tensor("lamv", [128, 128], F32, kind="ExternalInput").ap()
    gsub_d = nc.dram_tensor("gsub", [128, 64], F32, kind="ExternalInput").ap()
    out_d = nc.dram_tensor("out", [NSEQ_CORE * SEQ, DM], F32, kind="ExternalOutput").ap()
    wscr = nc.dram_tensor("wscr", [NCHUNK, 128, 4096], BF16).ap()

    S = Sched()
    cap_stack = []

    def OP(eng, fn, reads=(), writes=(), dsem=None):
        if dry and not cap_stack:
            return
        if cap_stack:
            cap_stack[-1].append((eng, fn, list(reads), list(writes), dsem))
        else:
            S.op(eng, fn, reads=reads, writes=writes, dsem=dsem)

    @contextlib.contextmanager
    def cap():
        lst = []
        cap_stack.append(lst)
        try:
            yield lst
        finally:
            cap_stack.pop()

    def replay(lst):
        if dry:
            return
        for (eng, fn, reads, writes, dsem) in lst:
            S.op(eng, fn, reads=reads, writes=writes, dsem=dsem)

    A = nc.alloc_sbuf_tensor
    xg = [A(f"xg{i}", [128, 4, DM], F32) for i in range(2)]
    actT = A("actT", [128, 8, GT], BF16)
    h2T = A("h2T", [128, 8, GT], BF16)
    hT = A("hT", [128, 8, GT], BF16)
    hbf = [A(f"hbf{i}", [128, DM], BF16) for i in range(2)]
    junk = A("junk", [128, DM], BF16)
    NW = 2
    W = [A(f"W{i}", [128, 8, 512], BF16) for i in range(NW)]
    kaT = A("kaT", [128, 4, SEQ], BF16)
    va = A("va", [128, 16, 8, 65], BF16)
    kbT = A("kbT", [128, 4, 1024], BF16)
    vb = A("vb", [128, 8, 8, 65], BF16)
    qaZ = A("qaZ", [128, 4, 4, GT], BF16)
    qbZ = A("qbZ", [128, 4, 2, GT], BF16)
    rmask = A("rmask_sb", [128, 6], F32)
    ropeo = [A(f"ropeo{i}", [128, 512], BF16) for i in range(2)]
    rt = [A(f"rt{i}", [128, 256], F32) for i in range(4)]
    PT = [A(f"PT{i}", [128, 512], BF16) for i in range(3)]
    obuf = [A(f"obuf{i}", [128, 8, 64], F32) for i in range(2)]
    sqo = A("sqo", [128, 8, 64], F32)
    mix = A("mix", [128, 2, DM], BF16)
    aT = [A(f"aT{i}", [128, 8, GT], BF16) for i in range(1)]
    sq = [A(f"sq{i}", [128, 512], F32) for i in range(2)]
    gbc12 = A("gbc12_sb", [128, 2, DM], BF16)
    gbcf = A("gbcf_sb", [128, DM], F32)
    ropec = A("ropec", [128, 16, 2, 16], F32)
    ident = A("ident_sb", [128, 128], BF16)
    biasT = A("biasT", [128, 8, 640], BF16)
    lamv = A("lamv_sb", [128, 4, 32], F32)
    lamt = A("lamt", [128, 2, 32], F32)
    lams = A("lams", [128, 8], F32)
    gsub = A("gsub_sb", [128, 64], F32)
    st = A("st", [128, 3, 4, 4], F32)
    rec = [A(f"rec{i}", [128, 2], F32) for i in range(2)]
    t1 = [A(f"t1_{i}", [128, 64], F32) for i in range(2)]
    recb = [A(f"recb{i}", [128, 1], F32) for i in range(2)]
    so = A("so", [128, 3, 8], F32)
    maskc = A("maskc", [128, 1], F32)

    PS = [nc.alloc_psum_tensor(f"psf{i}", [128, 512], F32) for i in (0, 1)]
    PS_TP = [nc.alloc_psum_tensor(f"pstp{i}", [128, 1024], BF16) for i in (0, 1)]
    PS_S = [nc.alloc_psum_tensor(f"pss{i}", [128, 512], F32) for i in (0, 1)]
    PS_ACC = [nc.alloc_psum_tensor(f"psa{i}", [128, 512], F32) for i in (0, 1)]

    cnt = {}

    def rot(name, n):
        v = cnt.get(name, 0) % n
        cnt[name] = cnt.get(name, 0) + 1
        return v

    def I(name, *args, **kwargs):
        return lambda e: getattr(e, name)(*args, **kwargs)

    OP("sp", I("dma_start", out=ident[:], in_=ident_d), writes=["ident"], dsem="dc0")
    OP("pool", I("dma_start", out=gbc12[:], in_=gbc_d[:, 0:2 * DM].rearrange("p (a d) -> p a d", a=2)),
       writes=["gbc12"], dsem="dc1")
    OP("sp", I("dma_start", out=gbcf[:], in_=gbc_d[:, 2 * DM:3 * DM]), writes=["gbcf"], dsem="dc6")
    OP("sp", I("dma_start", out=rmask[:], in_=rmask_d), writes=["rmask"], dsem="dc7")
    OP("sp", I("dma_start", out=ropec[:].rearrange("p a b c -> p (a b c)"), in_=rope_d), writes=["ropec"],
         dsem="dc2")
    OP("sp", I("dma_start", out=lamv[:].rearrange("p a d -> p (a d)"), in_=lamv_d), writes=["lamv"], dsem="dc3")
    OP("sp", I("dma_start", out=gsub[:], in_=gsub_d), writes=["gsub"], dsem="dc4")
    OP("pool", I("dma_start", out=biasT[:], in_=bias_d.rearrange("p (h q) -> p h q", h=8)), writes=["biasT"],
         dsem="dc5")
    OP("pool", I("memset", maskc[0:64, :], 0.0), writes=["maskc_lo"])
    OP("pool", I("memset", maskc[64:128, :], NEG), writes=["maskc"])
    OP("dve", I("tensor_scalar", out=biasT[:], in0=biasT[:], scalar1=8.0, scalar2=None, op0=ALU.mult),
       reads=["biasT"], writes=["bias8"])
    OP("pool", I("memset", va[:].rearrange("p a h e -> p (a h) e")[:, :, 64:65], 1.0), writes=["va_ones"])
    OP("pool", I("memset", vb[:].rearrange("p a h e -> p (a h) e")[:, :, 64:65], 1.0), writes=["vb_ones"])
    OP("dve", I("tensor_tensor", out=lamt[:, 0, :], in0=lamv[:, 0, :], in1=lamv[:, 1, :], op=ALU.mult),
         reads=["lamv"], writes=["lamt0"])
    OP("dve", I("tensor_tensor", out=lamt[:, 1, :], in0=lamv[:, 2, :], in1=lamv[:, 3, :], op=ALU.mult),
         reads=["lamv"], writes=["lamt1"])
    OP("dve", I("tensor_reduce", out=lams[:, 0:2], in_=lamt[:], axis=AX.X, op=ALU.add),
         reads=["lamt0", "lamt1"], writes=["lams01"])
    OP("act", I("activation", out=lams[:, 2:4], in_=lams[:, 0:2], func=AF.Exp), reads=["lams01"],
         writes=["lams23"])
    OP("dve", I("scalar_tensor_tensor", out=lams[:, 4:5], in0=lams[:, 3:4], scalar=-0.2, in1=lams[:, 2:3],
                  op0=ALU.add, op1=ALU.subtract), reads=["lams23"], writes=["neglam"])
    OP("dve", I("tensor_scalar", out=gsub[:], in0=gsub[:], scalar1=0.8, scalar2=None, op0=ALU.mult),
         reads=["gsub"], writes=["gsub"])
    neglam = lams[:, 4:5]

    for c in range(NCHUNK):
        OP("pool", I("dma_start", out=wscr[c].rearrange("p (a n) -> p a n", a=8),
                       in_=wall_d[c].rearrange("p (a n) -> p a n", a=8)),
             writes=[("wscr", c)], dsem=f"dw{c}")

    wstate = {"next": 0, "cons": 0}
    chunk_seq = [] if dry else list(chunk_seq_in)

    def issue_wload(n):
        c = chunk_seq[n]
        slot = n % NW
        OP("sp", I("dma_start", out=W[slot][:].rearrange("p a n -> p (a n)"), in_=wscr[c]),
           reads=[("wscr", c)], writes=[("W", slot)], dsem=f"dW{slot}")

    def get_chunk(c):
        n = wstate["cons"]
        wstate["cons"] += 1
        if dry:
            chunk_seq.append(c)
            return n % NW
        assert chunk_seq[n] == c, (n, chunk_seq[n], c)
        assert not cap_stack
        while wstate["next"] <= min(n + NW - 1, len(chunk_seq) - 1):
            issue_wload(wstate["next"])
            wstate["next"] += 1
        return n % NW

    def issue_xload(G):
        slot = G % 2
        src = x_d[G * GT:(G + 1) * GT, :].rearrange("(t p) d -> p t d", p=128)
        OP("sp", I("dma_start", out=xg[slot][:], in_=src),
             writes=[("xg", slot, t) for t in range(4)], dsem=f"dx{slot}")

    def transpose_masked(src, src_keys, dstZ, ngrp, mcol0, t, dkey):
        b = rot("tp", 2)
        for i in range(4):
            OP("pe", I("transpose", out=PS_TP[b][:, i * 128:(i + 1) * 128], in_=src[:, i * 128:(i + 1) * 128],
                       identity=ident[:]), reads=list(src_keys) + ["ident"], writes=[("pstp", b)])
        src_ps = PS_TP[b][:, 0:512].rearrange("p (a b) -> p a b", a=4)
        use_dve = rot("tm", 2) == 0
        for r in range(ngrp):
            dst = dstZ[:, :, r, t * 128:(t + 1) * 128]
            if use_dve:
                OP("dve", I("tensor_scalar", out=dst, in0=src_ps, scalar1=rmask[:, mcol0 + r:mcol0 + r + 1],
                            scalar2=None, op0=ALU.mult), reads=[("pstp", b), "rmask"], writes=[(dkey, t, r)])
            else:
                OP("act", I("mul", out=dst, in_=src_ps, mul=rmask[:, mcol0 + r:mcol0 + r + 1]),
                   reads=[("pstp", b), "rmask"], writes=[(dkey, t, r)])

    def transpose_to(src, src_keys, nblk, dst_ap, dst_keys, eng):
        b = rot("tp", 2)
        for i in range(nblk):
            OP("pe", I("transpose", out=PS_TP[b][:, i * 128:(i + 1) * 128], in_=src[:, i * 128:(i + 1) * 128],
                         identity=ident[:]), reads=list(src_keys) + ["ident"], writes=[("pstp", b)])
        src_ps = PS_TP[b][:, 0:nblk * 128].rearrange("p (a b) -> p a b", a=nblk)
        if eng == "act":
            OP("act", I("activation", out=dst_ap, in_=src_ps, func=AF.Copy), reads=[("pstp", b)], writes=dst_keys)
        else:
            OP("dve", I("tensor_copy", out=dst_ap, in_=src_ps), reads=[("pstp", b)], writes=dst_keys)

    def rms_stats(xs_t, xkey, kind, t, hs=None):
        OP("act", I("activation", out=junk[:], in_=xs_t, func=AF.Square, accum_out=st[:, kind, t, 0:1]),
             reads=[xkey], writes=["junk", ("st", kind, t, 0)])
        OP("act", I("activation", out=st[:, kind, t, 1:2], in_=st[:, kind, t, 0:1], func=AF.Ln,
                      scale=1.0 / DM, bias=EPS), reads=[("st", kind, t, 0)], writes=[("st", kind, t, 1)])
        OP("act", I("activation", out=st[:, kind, t, 2:3], in_=st[:, kind, t, 1:2], func=AF.Exp, scale=-0.5),
             reads=[("st", kind, t, 1)], writes=[("st", kind, t, 2)])

    def norm_to_actT(G, kind, t, dstT=None, dkey="actT"):
        dstT = actT if dstT is None else dstT
        xs_t = xg[G % 2][:, t, :]
        xkey = ("xg", G % 2, t)
        hs = rot("hbf", 2)
        rms_stats(xs_t, xkey, kind, t, hs)
        OP("dve", I("scalar_tensor_tensor", out=hbf[hs][:], in0=xs_t, scalar=st[:, kind, t, 2:3],
                      in1=gbc12[:, kind, :], op0=ALU.mult, op1=ALU.mult),
             reads=[xkey, ("st", kind, t, 2), "gbc12"], writes=[("hbf", hs)])
        transpose_to(hbf[hs], [("hbf", hs)], 8, dstT[:, :, t * 128:(t + 1) * 128], [(dkey, t)], "act")

    def norm_phase(G, kind, dstT, dkey, filler=None, nfill=0):
        xs = xg[G % 2]
        for t in range(4):
            rms_stats(xs[:, t, :], ("xg", G % 2, t), kind, t)
        pend = None
        for t in range(4):
            hs = rot("hbf", 2)
            OP("dve", I("scalar_tensor_tensor", out=hbf[hs][:], in0=xs[:, t, :], scalar=st[:, kind, t, 2:3],
                        in1=gbc12[:, kind, :], op0=ALU.mult, op1=ALU.mult),
               reads=[("xg", G % 2, t), ("st", kind, t, 2), "gbc12"], writes=[("hbf", hs)])
            if pend is not None:
                replay(pend)
            if filler is not None:
                for _ in range(nfill):
                    next(filler, None)
            b = rot("tp", 2)
            for i in range(8):
                OP("pe", I("transpose", out=PS_TP[b][:, i * 128:(i + 1) * 128], in_=hbf[hs][:, i * 128:(i + 1) * 128],
                           identity=ident[:]), reads=[("hbf", hs), "ident"], writes=[("pstp", b)])
            with cap() as pend:
                OP("dve", I("tensor_copy", out=dstT[:, :, t * 128:(t + 1) * 128],
                            in_=PS_TP[b][:, :].rearrange("p (a b) -> p a b", a=8)),
                   reads=[("pstp", b)], writes=[(dkey, t)])
        replay(pend)

    def dense_tokmajor(wslot, lhs_list, lhs_keys):
        b = rot("mm", 2)
        nk = len(lhs_list)
        for k in range(nk):
            OP("pe", I("matmul", PS[b][:, :], lhsT=lhs_list[k], rhs=W[wslot][:, k, :], start=(k == 0),
                         stop=(k == nk - 1)), reads=list(lhs_keys) + [("W", wslot)], writes=[("psf", b)])
        return b

    def residual_add(G, b, t, dh):
        xs = xg[G % 2]
        OP("dve", I("tensor_tensor", out=xs[:, t, dh * 512:(dh + 1) * 512], in0=PS[b][:, :],
                      in1=xs[:, t, dh * 512:(dh + 1) * 512], op=ALU.add),
             reads=[("psf", b), ("xg", G % 2, t)], writes=[("xg", G % 2, t)])

    def inproj(G, cg, ws, t):
        gs = G % NG_SEQ
        jt = gs * 4 + t
        b = dense_tokmajor(ws, [hT[:, k, t * 128:(t + 1) * 128] for k in range(8)], [("hT", t)])
        ps = PS[b]
        if cg in (0, 1):
            v = ps[:, :].rearrange("p (s h j) -> p s h j", s=16, h=2, j=16)
            x1, x2 = v[:, :, 0, :], v[:, :, 1, :]
            cosb = ropec[:, jt, 0, :].unsqueeze(1).to_broadcast([128, 16, 16])
            sinb = ropec[:, jt, 1, :].unsqueeze(1).to_broadcast([128, 16, 16])
            r3 = [r[:].rearrange("p (s j) -> p s j", s=16) for r in rt]
            for i, (a_, b_) in enumerate([(x1, cosb), (x2, sinb), (x2, cosb), (x1, sinb)]):
                OP("dve", I("tensor_tensor", out=r3[i], in0=a_, in1=b_, op=ALU.mult),
                     reads=[("psf", b), "ropec"], writes=[f"rt{i}"])
            ro = rot("ropeo", 2)
            rov = ropeo[ro][:].rearrange("p (s h j) -> p s h j", s=16, h=2, j=16)
            OP("pool", I("tensor_tensor", out=rov[:, :, 0, :], in0=r3[0], in1=r3[1], op=ALU.subtract),
                 reads=["rt0", "rt1"], writes=[("ropeo", ro, 0)])
            OP("pool", I("tensor_tensor", out=rov[:, :, 1, :], in0=r3[2], in1=r3[3], op=ALU.add),
                 reads=["rt2", "rt3"], writes=[("ropeo", ro, 1)])
            with cap() as deferred:
                if cg == 0:
                    transpose_masked(ropeo[ro], [("ropeo", ro, 0), ("ropeo", ro, 1)], qaZ, 4, 0, t, "qaZ")
                else:
                    dst, dk = kaT[:, :, jt * 128:(jt + 1) * 128], [("kaT", jt)]
                    transpose_to(ropeo[ro], [("ropeo", ro, 0), ("ropeo", ro, 1)], 4, dst, dk, "dve")
            return deferred
        elif cg in (3, 4):
            ro = rot("ropeo", 2)
            OP("act", I("activation", out=ropeo[ro][:], in_=ps[:, :], func=AF.Copy),
                 reads=[("psf", b)], writes=[("ropeo", ro, 0), ("ropeo", ro, 1)])
            with cap() as deferred:
                if cg == 3:
                    transpose_masked(ropeo[ro], [("ropeo", ro, 0), ("ropeo", ro, 1)], qbZ, 2, 4, t, "qbZ")
                else:
                    sl = jt % 8
                    dst, dk = kbT[:, :, sl * 128:(sl + 1) * 128], [("kbT", sl)]
                    transpose_to(ropeo[ro], [("ropeo", ro, 0), ("ropeo", ro, 1)], 4, dst, dk, "dve")
            return deferred
        else:
            pv = ps[:, :].rearrange("p (h d) -> p h d", h=8)
            if cg == 2:
                dst, dk = va[:, jt, :, 0:64], [("va", jt)]
            else:
                sl = jt % 8
                dst, dk = vb[:, sl, :, 0:64], [("vb", sl)]
            OP("act", I("activation", out=dst, in_=pv, func=AF.Copy), reads=[("psf", b)], writes=dk)
            return []

    def attn_units(G):
        units = []
        for t in range(4):
            qb = (G % NG_SEQ) * 4 + t
            ob = rot("ob", 2)
            jlo = max(0, qb - 4)

            def a_head(h, t=t, qb=qb, ob=ob, jlo=jlo):
                ab = rot("acc", 2)
                njt = qb + 1
                for c in range(2):
                    s = 2 * h + c
                    rg, ti = s % 4, s // 4
                    for j0 in range(0, njt, 4):
                        n = min(4, njt - j0)
                        sb = rot("s", 2)
                        pt = rot("pt", 3)
                        u = {}
                        with cap() as u["S"]:
                            for jj in range(n):
                                j = j0 + jj
                                OP("pe", I("matmul", PS_S[sb][:, jj * 128:(jj + 1) * 128],
                                           lhsT=kaT[:, ti, j * 128:(j + 1) * 128],
                                           rhs=qaZ[:, ti, rg, t * 128:(t + 1) * 128], start=True, stop=True),
                                   reads=[("kaT", j), ("qaZ", t, rg)], writes=[("pss", sb)])
                        with cap() as u["post"]:
                            OP("act", I("activation", out=PT[pt][:, 0:n * 128], in_=PS_S[sb][:, 0:n * 128],
                                        func=AF.Exp, scale=32 ** -0.5), reads=[("pss", sb)], writes=[("PT", pt)])
                            if j0 + n == njt:
                                c0 = (n - 1) * 128
                                OP("act", I("activation", out=PT[pt][:, c0:c0 + 64], in_=PS_S[sb][:, c0:c0 + 64],
                                            func=AF.Exp, scale=32 ** -0.5, bias=maskc[:, 0:1]),
                                   reads=[("pss", sb), "maskc"], writes=[("PT", pt)])
                        with cap() as u["PV"]:
                            for jj in range(n):
                                j = j0 + jj
                                OP("pe", I("matmul", PS_ACC[ab][:, c * 65:(c + 1) * 65],
                                           lhsT=PT[pt][:, jj * 128:(jj + 1) * 128], rhs=va[:, j, h, :],
                                           start=(j == 0), stop=(j == njt - 1)),
                                   reads=[("PT", pt), ("va", j), "va_ones"], writes=[("psa", ab)])
                        with cap() as u["fin"]:
                            if c == 1 and j0 + n == njt:
                                rc = rot("rec", 2)
                                accv = PS_ACC[ab][:, 0:130].rearrange("p (c e) -> p c e", c=2)
                                OP("dve", I("reciprocal", out=rec[rc][:], in_=accv[:, :, 64]), reads=[("psa", ab)],
                                   writes=[("rec", rc)])
                                OP("dve", I("tensor_scalar", out=t1[rc][:], in0=PS_ACC[ab][:, 65:129],
                                            scalar1=rec[rc][:, 1:2], scalar2=neglam, op0=ALU.mult, op1=ALU.mult),
                                   reads=[("psa", ab), ("rec", rc), "neglam"], writes=[("t1", rc)])
                                OP("dve", I("scalar_tensor_tensor", out=obuf[ob][:, h, :], in0=PS_ACC[ab][:, 0:64],
                                            scalar=rec[rc][:, 0:1], in1=t1[rc][:], op0=ALU.mult, op1=ALU.add),
                                   reads=[("psa", ab), ("rec", rc), ("t1", rc)], writes=[("obuf", ob)])
                                if h == 7:
                                    OP("pool", I("tensor_tensor", out=sqo[:], in0=obuf[ob][:], in1=obuf[ob][:],
                                                 op=ALU.mult), reads=[("obuf", ob)], writes=["sqo"])
                                    OP("dve", I("tensor_reduce", out=so[:, 0, :], in_=sqo[:], axis=AX.X, op=ALU.add),
                                       reads=["sqo"], writes=["so0"])
                        if h == 7 and c == 1 and j0 + n == njt:
                            with cap() as u["late1"]:
                                OP("act", I("activation", out=so[:, 1, :], in_=so[:, 0, :], func=AF.Ln,
                                            scale=1.0 / 64, bias=EPS), reads=["so0"], writes=["so1"])
                                OP("act", I("activation", out=so[:, 2, :], in_=so[:, 1, :], func=AF.Exp,
                                            scale=-0.5), reads=["so1"], writes=["so2"])
                                OP("pool", I("tensor_tensor", out=sqo[:], in0=obuf[ob][:],
                                             in1=so[:, 2, :].unsqueeze(2).to_broadcast([128, 8, 64]), op=ALU.mult),
                                   reads=[("obuf", ob), "so2"], writes=["sqo"])
                                OP("pool", I("tensor_tensor",
                                             out=mix[:, t % 2, 0:512].rearrange("p (h d) -> p h d", h=8), in0=sqo[:],
                                             in1=gsub[:].unsqueeze(1).to_broadcast([128, 8, 64]), op=ALU.mult),
                                   reads=["sqo", "gsub"], writes=[("mixA", t % 2)])
                        units.append(u)

            def b_head(h, t=t, qb=qb, ob=ob, jlo=jlo):
                ab = rot("acc", 2)
                rg, ti = h % 2, h // 2
                njt = qb - jlo + 1
                for j0 in range(0, njt, 4):
                    n = min(4, njt - j0)
                    sb = rot("s", 2)
                    pt = rot("pt", 3)
                    u = {}
                    with cap() as u["S"]:
                        jb = (jlo + j0) - qb + 4
                        OP("pe", I("matmul", PS_S[sb][:, 0:n * 128], lhsT=ident[:],
                                   rhs=biasT[:, h, jb * 128:(jb + n) * 128], start=True, stop=False),
                           reads=["ident", "bias8"], writes=[("pss", sb)])
                        for jj in range(n):
                            sl = (jlo + j0 + jj) % 8
                            OP("pe", I("matmul", PS_S[sb][:, jj * 128:(jj + 1) * 128],
                                       lhsT=kbT[:, ti, sl * 128:(sl + 1) * 128],
                                       rhs=qbZ[:, ti, rg, t * 128:(t + 1) * 128], start=False, stop=(jj == n - 1)),
                               reads=[("kbT", sl), ("qbZ", t, rg)], writes=[("pss", sb)])
                    with cap() as u["post"]:
                        OP("act", I("activation", out=PT[pt][:, 0:n * 128], in_=PS_S[sb][:, 0:n * 128], func=AF.Exp,
                                    scale=0.125), reads=[("pss", sb)], writes=[("PT", pt)])
                    with cap() as u["PV"]:
                        for jj in range(n):
                            j = jlo + j0 + jj
                            sl = j % 8
                            OP("pe", I("matmul", PS_ACC[ab][:, 0:65], lhsT=PT[pt][:, jj * 128:(jj + 1) * 128],
                                       rhs=vb[:, sl, h, :], start=(j == jlo), stop=(j == qb)),
                               reads=[("PT", pt), ("vb", sl), "vb_ones"], writes=[("psa", ab)])
                    with cap() as u["fin"]:
                        if j0 + n == njt:
                            rb = rot("recb", 2)
                            OP("dve", I("reciprocal", out=recb[rb][:], in_=PS_ACC[ab][:, 64:65]),
                               reads=[("psa", ab)], writes=[("recb", rb)])
                            OP("dve", I("tensor_scalar", out=mix[:, t % 2, 512 + h * 64:512 + (h + 1) * 64],
                                        in0=PS_ACC[ab][:, 0:64], scalar1=recb[rb][:, 0:1], scalar2=None,
                                        op0=ALU.mult), reads=[("psa", ab), ("recb", rb)], writes=[("mixB", t % 2)])
                    if h == 7 and j0 + n == njt:
                        with cap() as u["late"]:
                            transpose_to(mix[:, t % 2, :], [("mixA", t % 2), ("mixB", t % 2)], 8,
                                         actT[:, :, t * 128:(t + 1) * 128], [("actT", t)], "act")
                    units.append(u)

            for h in range(8):
                a_head(h)
                b_head(h)
        return units

    def ffn_gen(Gp):
        order = [(1, 0, 8), (2, 0, 10), (1, 1, 12), (2, 1, 14), (1, 2, 16), (2, 2, 18), (1, 3, 20), (2, 3, 22)]
        for kind, r, cidx in order:
            if kind == 1:
                for half in range(2):
                    ws = get_chunk(cidx + half)
                    for f4 in range(4):
                        f8 = half * 4 + f4
                        b = rot("mm", 2)
                        for k in range(8):
                            OP("pe", I("matmul", PS[b][:, :], lhsT=W[ws][:, k, f4 * 128:(f4 + 1) * 128],
                                       rhs=h2T[:, k, :], start=(k == 0), stop=(k == 7)),
                               reads=[("W", ws)] + [("h2T", tt) for tt in range(4)], writes=[("psf", b)])
                            if k == 7:
                                sqs = rot("sq", 2)
                                OP("dve", I("tensor_scalar", out=sq[sqs][:], in0=PS[b][:, :], scalar1=0.0,
                                            scalar2=None, op0=ALU.max), reads=[("psf", b)], writes=[("sq", sqs)])
                                OP("pool", I("tensor_tensor", out=aT[0][:, f8, :], in0=sq[sqs][:],
                                             in1=sq[sqs][:], op=ALU.mult), reads=[("sq", sqs)],
                                   writes=[("aT", 0, f8)])
                            yield
            else:
                for dh in range(2):
                    ws = get_chunk(cidx + dh)
                    for t in range(4):
                        b = rot("mm", 2)
                        for k in range(8):
                            OP("pe", I("matmul", PS[b][:, :], lhsT=aT[0][:, k, t * 128:(t + 1) * 128],
                                       rhs=W[ws][:, k, :], start=(k == 0), stop=(k == 7)),
                               reads=[("aT", 0, k), ("W", ws)], writes=[("psf", b)])
                            if k == 7:
                                residual_add(Gp, b, t, dh)
                            yield
        final_norm_store(Gp)
        if Gp + 2 < NGT:
            issue_xload(Gp + 2)
        yield

    def final_norm_store(G):
        xs = xg[G % 2]
        for t in range(4):
            rms_stats(xs[:, t, :], ("xg", G % 2, t), 2, t)
        for t in range(4):
            OP("dve", I("scalar_tensor_tensor", out=xs[:, t, :], in0=xs[:, t, :], scalar=st[:, 2, t, 2:3],
                        in1=gbcf[:], op0=ALU.mult, op1=ALU.mult),
               reads=[("xg", G % 2, t), ("st", 2, t, 2), "gbcf"], writes=[("xg", G % 2, t)])
        dst = out_d[G * GT:(G + 1) * GT, :].rearrange("(t p) d -> p t d", p=128)
        OP("pool", I("dma_start", out=dst, in_=xs[:]), reads=[("xg", G % 2, t) for t in range(4)],
           dsem=f"do{G % 2}")

    def outproj_gen(G):
        for dh in range(2):
            ws = get_chunk(6 + dh)
            for t in range(4):
                b = rot("mm", 2)
                for k in range(8):
                    OP("pe", I("matmul", PS[b][:, :], lhsT=actT[:, k, t * 128:(t + 1) * 128], rhs=W[ws][:, k, :],
                               start=(k == 0), stop=(k == 7)), reads=[("actT", t), ("W", ws)], writes=[("psf", b)])
                    if k == 7:
                        residual_add(G, b, t, dh)
                    yield

    def inproj_gen(G, cgs):
        pending = []
        for cg in cgs:
            ws = get_chunk(cg)
            for t in range(4):
                nxt = inproj(G, cg, ws, t)
                replay(pending)
                pending = nxt
                for _ in range(8):
                    yield
        replay(pending)

    def drain(g):
        for _ in g:
            pass

    issue_xload(0)
    if NGT > 1:
        issue_xload(1)
    N_DENSE = 513
    norm_phase(0, 0, hT, "hT")
    drain(inproj_gen(0, [2, 5, 0, 1, 3, 4]))
    dense = iter(())
    n_left = 0
    for G in range(NGT):
        units = attn_units(G)
        n = len(units)
        replay(units[0]["S"])
        accf = 0.0
        rate = n_left * 1.15 / n
        LATE = 10
        LATE1 = 5
        for ui in range(n):
            if ui + 1 < n:
                replay(units[ui + 1]["S"])
            replay(units[ui]["post"])
            accf += rate
            while accf >= 1.0:
                accf -= 1.0
                next(dense, None)
            replay(units[ui]["PV"])
            replay(units[ui]["fin"])
            if ui >= LATE1 and "late1" in units[ui - LATE1]:
                replay(units[ui - LATE1]["late1"])
            if ui >= LATE and "late" in units[ui - LATE]:
                replay(units[ui - LATE]["late"])
        for ui in range(max(0, n - LATE1), n):
            if "late1" in units[ui]:
                replay(units[ui]["late1"])
        for ui in range(max(0, n - LATE), n):
            if "late" in units[ui]:
                replay(units[ui]["late"])
        drain(dense)
        og = outproj_gen(G)
        if G + 1 < NGT:
            norm_phase(G + 1, 0, hT, "hT", og, 16)
        drain(og)
        if G + 1 < NGT:
            ig = inproj_gen(G + 1, [2, 5])
            norm_phase(G, 1, h2T, "h2T", ig, 16)
            drain(ig)
            drain(inproj_gen(G + 1, [0, 1, 3, 4]))
        else:
            norm_phase(G, 1, h2T, "h2T")
        dense = ffn_gen(G)
        n_left = N_DENSE
    drain(dense)
    if dry:
        return chunk_seq

    S.finalize()
    dsem_names = sorted(S.dsem_count.keys())
    with contextlib.ExitStack() as es:
        sems = {e: [es.enter_context(nc.semaphore(f"s_{e}{i}")) for i in range(S.nepoch[e])]
                for e in Sched.ENGS}
        dsems = {k: es.enter_context(nc.semaphore(f"ds_{k}")) for k in dsem_names}
        block = es.enter_context(nc.Block())
        finals = [(dsems[k], S.dsem_count[k]) for k in ("do0", "do1") if k in S.dsem_count]

        @block.sync
        def _(eng):
            S.emit("sp", eng, sems, dsems)

        @block.scalar
        def _(eng):
            S.emit("act", eng, sems, dsems)

        @block.vector
        def _(eng):
            S.emit("dve", eng, sems, dsems)

        @block.gpsimd
        def _(eng):
            S.emit("pool", eng, sems, dsems, final_waits=finals)

        @block.tensor
        def _(eng):
            S.emit("pe", eng, sems, dsems)
    return nc


def _chunk_k(w, c0):
    return np.ascontiguousarray(w[:, c0:c0 + 512].reshape(8, 128, 512).transpose(1, 0, 2)).reshape(128, 4096)


def _prep_shared(w_in, w_out, norm1_g, norm2_g, final_g, subln_g, lambda_q1, lambda_k1, lambda_q2, lambda_k2,
                 rel_bias, w_ff1, w_ff2):
    w_in, w_out, w_ff1, w_ff2 = (np.asarray(a, np.float32)[0] for a in (w_in, w_out, w_ff1, w_ff2))
    chunks = [_chunk_k(w_in, c * 512) for c in range(6)] + [_chunk_k(w_out, c * 512) for c in range(2)]

    def f1(r):
        return [_chunk_k(w_ff1, r * 1024 + hh * 512) for hh in range(2)]

    def f2(r):
        return [_chunk_k(w_ff2[r * 1024:(r + 1) * 1024], dh * 512) for dh in range(2)]

    chunks += f1(0) + f2(0) + f1(1) + f2(1) + f1(2) + f2(2) + f1(3) + f2(3)
    wall = np.stack(chunks, 0)
    gbc = np.concatenate([np.broadcast_to(np.asarray(g, np.float32).reshape(1, DM), (128, DM))
                          for g in (norm1_g, norm2_g, final_g)], axis=1)
    half = 16
    inv_freq = (np.float32(10000.0) ** (-np.arange(half, dtype=np.float32) / np.float32(half))).astype(np.float32)
    pos = np.arange(SEQ, dtype=np.float32)
    ang = pos[:, None] * inv_freq[None, :]
    cs = np.stack([np.cos(ang), np.sin(ang)], axis=1).astype(np.float32)
    rope = np.ascontiguousarray(cs.reshape(16, 128, 2, 16).transpose(1, 0, 2, 3)).reshape(128, 512)
    ident = np.eye(128, dtype=np.float32).astype(ml_dtypes.bfloat16)
    tab = np.concatenate([np.asarray(rel_bias, np.float32)[0], np.full((8, 1), NEG, np.float32)], axis=1)
    k = np.arange(128)[:, None, None]
    jb = np.arange(5)[None, :, None]
    q = np.arange(128)[None, None, :]
    r = 4 - jb
    idx = np.clip(r * 128 + q - k, -256, 256) + 256
    masked = ((r == 0) & (q < 64) & (k >= 64)) | ((r == 4) & (q >= 64) & (k < 64))
    idx = np.where(masked, 513, idx)
    biasg = np.ascontiguousarray(tab[:, idx].transpose(1, 0, 2, 3)).reshape(128, 8 * 5 * 128)
    lamv = np.concatenate([np.broadcast_to(np.asarray(v, np.float32).reshape(1, 32), (128, 32))
                           for v in (lambda_q1, lambda_k1, lambda_q2, lambda_k2)], axis=1)
    gsub = np.ascontiguousarray(np.broadcast_to(np.asarray(subln_g, np.float32).reshape(1, 64), (128, 64)))
    rmask = np.zeros((128, 6), np.float32)
    for r in range(4):
        rmask[32 * r:32 * r + 32, r] = 1.0
    for r in range(2):
        rmask[64 * r:64 * r + 64, 4 + r] = 1.0
    return {"rmask": rmask, "wall": wall, "gbc": np.ascontiguousarray(gbc), "rope": rope, "ident": ident, "biasg": biasg,
            "lamv": np.ascontiguousarray(lamv), "gsub": gsub}


_NC_CACHE = {}


def kernel(x, w_in, w_out, norm1_g, norm2_g, final_g, subln_g, lambda_q1, lambda_k1, lambda_q2, lambda_k2,
           rel_bias, w_ff1, w_ff2, _ngt=NGT_FULL):
    x = np.asarray(x, np.float32)
    shared = _prep_shared(w_in, w_out, norm1_g, norm2_g, final_g, subln_g, lambda_q1, lambda_k1, lambda_q2,
                          lambda_k2, rel_bias, w_ff1, w_ff2)
    if _ngt not in _NC_CACHE:
        _NC_CACHE[_ngt] = build(_ngt, build(_ngt))
    nc = _NC_CACHE[_ngt]
    in_maps = []
    for c in range(N_CORES):
        m = dict(shared)
        m["x"] = np.ascontiguousarray(x[c * NSEQ_CORE:(c + 1) * NSEQ_CORE].reshape(NSEQ_CORE * SEQ, DM))
        in_maps.append(m)
    res = run_bass_kernel_spmd(nc, in_maps, core_ids=list(range(N_CORES)))
    out = np.stack([np.asarray(r["out"], np.float32).reshape(NSEQ_CORE, SEQ, DM) for r in res.results], 0)
    return out.reshape(N_CORES * NSEQ_CORE, SEQ, DM)
```
